# Optimizing a Trainium2 kernel written in Bass

```python
import jax, jax.numpy as jnp
from jax import lax
import numpy as np

D_MODEL = 1024
BATCH = 32
SEQ = 256
DEPTH = 1
DEC_BATCH = 4
DEC_SEQ = 1024
PAST_LEN = 512

GRID_W = 64
N_HEADS = 4
HEAD_K = 128
HEAD_V = 256
KEY_DIM = N_HEADS * HEAD_K
VAL_DIM = N_HEADS * HEAD_V
GATE_RANK = 16
GATE_NORMALIZER = 16.0
CHUNK = 64
CONV_DIM = D_MODEL
D_FF = 2816
N_MOD = 6
EPS = 1e-6
IN_SPLITS = (KEY_DIM, KEY_DIM, VAL_DIM, VAL_DIM, GATE_RANK, GATE_RANK,
             CONV_DIM, CONV_DIM, CONV_DIM, VAL_DIM, CONV_DIM)
IN_DIM = 2 * KEY_DIM + 3 * VAL_DIM + 2 * GATE_RANK + 4 * CONV_DIM

kernel_name = "bidir_gla_shortconv_convffn_prefix_dit"


def rmsnorm(x, g):
    xf = x.astype(jnp.float32)
    y = xf * lax.rsqrt(jnp.mean(xf * xf, axis=-1, keepdims=True) + EPS)
    return (y * g.astype(jnp.float32)).astype(x.dtype)


def split_cols(h):
    outs, off = [], 0
    for w in IN_SPLITS:
        outs.append(h[..., off:off + w])
        off += w
    return outs


def to_heads(t, hd):
    b, l, _ = t.shape
    return t.reshape(b, l, N_HEADS, hd).transpose(0, 2, 1, 3)


def short_conv1d(x, w):
    l = x.shape[1]
    xp = jnp.pad(x, ((0, 0), (1, 1), (0, 0)))
    return xp[:, :l] * w[0] + xp[:, 1:l + 1] * w[1] + xp[:, 2:] * w[2]


def dwconv3x3(x, w, b, rows, width):
    bn, l, f = x.shape
    img = x.reshape(bn, rows, width, f)
    out = lax.conv_general_dilated(img, w[:, :, None, :].astype(img.dtype), (1, 1), ((1, 1), (1, 1)),
                                   dimension_numbers=('NHWC', 'HWIO', 'NHWC'), feature_group_count=f)
    return out.reshape(bn, l, f) + b


def gla_chunked(q, k, v, g, s0):
    bn, h, l, dk = q.shape
    dv = v.shape[-1]
    n = l // CHUNK
    qc = q.reshape(bn, h, n, CHUNK, dk)
    kc = k.reshape(bn, h, n, CHUNK, dk)
    vc = v.reshape(bn, h, n, CHUNK, dv)
    bcum = jnp.cumsum(g.astype(jnp.float32).reshape(bn, h, n, CHUNK, dk), axis=3)
    b_last = bcum[:, :, :, -1:, :]
    b_ref = bcum[:, :, :, CHUNK // 2:CHUNK // 2 + 1, :]
    a = jnp.einsum('bhncd,bhnsd->bhncs', qc * jnp.exp(bcum - b_ref), kc * jnp.exp(b_ref - bcum))
    tril = jnp.tril(jnp.ones((CHUNK, CHUNK), dtype=bool))
    a = jnp.where(tril, a, 0.0)
    o_intra = jnp.einsum('bhncs,bhnsv->bhncv', a, vc)
    u = jnp.einsum('bhncd,bhncv->bhndv', kc * jnp.exp(b_last - bcum), vc).astype(jnp.float32)
    decay = jnp.exp(b_last[:, :, :, 0, :])

    def step(s, inp):
        dec, uu = inp
        return dec[..., None] * s + uu, s

    s_final, s_starts = lax.scan(step, s0.astype(jnp.float32),
                                 (jnp.moveaxis(decay, 2, 0), jnp.moveaxis(u, 2, 0)))
    s_starts = jnp.moveaxis(s_starts, 0, 2)
    o_inter = jnp.einsum('bhncd,bhndv->bhncv', qc * jnp.exp(bcum), s_starts)
    o = (o_inter + o_intra).reshape(bn, h, l, dv)
    return o.astype(v.dtype), s_final.astype(v.dtype)


def gla_bidir(q, k, v, g_f, g_b, s0_f, s0_b):
    o_f, s_f = gla_chunked(q, k, v, g_f, s0_f)
    flip = lambda t: t[:, :, ::-1]
    o_b, s_b = gla_chunked(flip(q), flip(k), flip(v), flip(g_b), s0_b)
    return o_f + flip(o_b), s_f, s_b


def block(x, cvec, s0_f, s0_b, rows, width,
          w_ada, b_ada, norm1_g, w_in, w_gk_f, b_gk_f, w_gk_b, b_gk_b, gla_norm_g,
          conv_mix_w, w_out, norm2_g, ffn_w_up, ffn_w_gate, ffn_conv_w, ffn_conv_b, ffn_w_down):
    bn, l, _ = x.shape
    mod = (jax.nn.silu(cvec) @ w_ada + b_ada).reshape(cvec.shape[0], 1, N_MOD, D_MODEL)
    sh1, sc1, ga1, sh2, sc2, ga2 = [mod[:, :, i] for i in range(N_MOD)]
    xn = rmsnorm(x, norm1_g) * (1.0 + sc1) + sh1
    q, k, v, g_out, code_f, code_b, c_b, c_c, c_x, gate_a, gate_b = split_cols(xn @ w_in)
    g_f = jax.nn.log_sigmoid((code_f @ w_gk_f + b_gk_f).astype(jnp.float32)) / GATE_NORMALIZER
    g_b = jax.nn.log_sigmoid((code_b @ w_gk_b + b_gk_b).astype(jnp.float32)) / GATE_NORMALIZER
    o, s_f, s_b = gla_bidir(to_heads(q, HEAD_K) * (HEAD_K ** -0.5), to_heads(k, HEAD_K), to_heads(v, HEAD_V),
                            to_heads(g_f, HEAD_K), to_heads(g_b, HEAD_K), s0_f, s0_b)
    o = rmsnorm(o, gla_norm_g).transpose(0, 2, 1, 3).reshape(bn, l, VAL_DIM) * jax.nn.silu(g_out)
    o_conv = c_b * short_conv1d(c_c * c_x, conv_mix_w)
    mix = (jax.nn.sigmoid(gate_a) * o + jax.nn.sigmoid(gate_b) * o_conv) @ w_out
    x = x + ga1 * mix
    xn2 = rmsnorm(x, norm2_g) * (1.0 + sc2) + sh2
    hup = dwconv3x3(xn2 @ ffn_w_up, ffn_conv_w, ffn_conv_b, rows, width)
    y = (jax.nn.silu(hup) * (xn2 @ ffn_w_gate)) @ ffn_w_down
    x = x + ga2 * y
    return x, s_f, s_b


def setup_inputs(seed: int = 0) -> dict:
    key = jax.random.key(seed)
    ks = jax.random.split(key, 26)
    nrm = lambda i, shape, s: jax.random.normal(ks[i], shape, jnp.float32) * s
    d = D_MODEL
    st_shape = (DEC_BATCH, DEPTH, N_HEADS, HEAD_K, HEAD_V)
    return {
        'x_prompt': nrm(0, (BATCH, SEQ, d), 1.0),
        'x_sample': nrm(1, (DEC_BATCH, DEC_SEQ, d), 1.0),
        'c': nrm(2, (DEC_BATCH, d), 1.0),
        'state_gla_fwd': nrm(3, st_shape, 1.0),
        'state_gla_bwd': nrm(4, st_shape, 1.0),
        'c_ctx': nrm(5, (d,), 1.0),
        'w_ada': nrm(6, (DEPTH, d, N_MOD * d), 0.5 * d ** -0.5),
        'b_ada': nrm(7, (DEPTH, N_MOD * d), 0.02),
        'norm1_g': 1.0 + nrm(8, (DEPTH, d), 0.02),
        'w_in': nrm(9, (DEPTH, d, IN_DIM), d ** -0.5),
        'w_gk_f': nrm(10, (DEPTH, GATE_RANK, KEY_DIM), GATE_RANK ** -0.5),
        'b_gk_f': nrm(11, (DEPTH, KEY_DIM), 0.1),
        'w_gk_b': nrm(12, (DEPTH, GATE_RANK, KEY_DIM), GATE_RANK ** -0.5),
        'b_gk_b': nrm(13, (DEPTH, KEY_DIM), 0.1),
        'gla_norm_g': 1.0 + nrm(14, (DEPTH, HEAD_V), 0.02),
        'conv_mix_w': nrm(15, (DEPTH, 3, CONV_DIM), 3 ** -0.5),
        'w_out': nrm(16, (DEPTH, VAL_DIM, d), VAL_DIM ** -0.5),
        'norm2_g': 1.0 + nrm(17, (DEPTH, d), 0.02),
        'ffn_w_up': nrm(18, (DEPTH, d, D_FF), d ** -0.5),
        'ffn_w_gate': nrm(19, (DEPTH, d, D_FF), d ** -0.5),
        'ffn_conv_w': nrm(20, (DEPTH, 3, 3, D_FF), 1.0 / 3.0),
        'ffn_conv_b': nrm(21, (DEPTH, D_FF), 0.02),
        'ffn_w_down': nrm(22, (DEPTH, D_FF, d), D_FF ** -0.5),
        'normf_g': 1.0 + nrm(23, (d,), 0.02),
    }


def reference(x_prompt, x_sample, c, state_gla_fwd, state_gla_bwd, c_ctx, w_ada, b_ada, norm1_g, w_in,
              w_gk_f, b_gk_f, w_gk_b, b_gk_b, gla_norm_g, conv_mix_w, w_out, norm2_g,
              ffn_w_up, ffn_w_gate, ffn_conv_w, ffn_conv_b, ffn_w_down, normf_g):
    bp, lp, _ = x_prompt.shape
    ls = x_sample.shape[1]
    rows = ls // GRID_W
    zero_state = jnp.zeros((bp, N_HEADS, HEAD_K, HEAD_V), x_prompt.dtype)
    xp, xs = x_prompt, x_sample
    new_f, new_b = [], []
    for l in range(DEPTH):
        p = (w_ada[l], b_ada[l], norm1_g[l], w_in[l], w_gk_f[l], b_gk_f[l], w_gk_b[l], b_gk_b[l],
             gla_norm_g[l], conv_mix_w[l], w_out[l], norm2_g[l], ffn_w_up[l], ffn_w_gate[l],
             ffn_conv_w[l], ffn_conv_b[l], ffn_w_down[l])
        xp, s_f, s_b = block(xp, c_ctx[None, :], zero_state, zero_state, 1, lp, *p)
        new_f.append(s_f)
        new_b.append(s_b)
        xs, _, _ = block(xs, c, state_gla_fwd[:, l], state_gla_bwd[:, l], rows, GRID_W, *p)
    y_prompt = rmsnorm(xp, normf_g)
    y_sample = rmsnorm(xs, normf_g)
    return (y_prompt, y_sample, jnp.stack(new_f, axis=1), jnp.stack(new_b, axis=1))
```

```python
import numpy as np
from contextlib import ExitStack
import concourse.bass as bass
import concourse.mybir as mybir
from concourse.bass_utils import run_bass_kernel_spmd

F32 = mybir.dt.float32
BF16 = mybir.dt.bfloat16
AF = mybir.ActivationFunctionType
ALU = mybir.AluOpType
ND = 6

NT = 12
NTOK = 1536
DFF = 2816
NFC = 22
EPS = 1e-6
C_Q, C_K, C_V, C_G, C_CF, C_CB, C_B, C_C, C_X, C_GA, C_GB = 0, 512, 1024, 2048, 3072, 3088, 3104, 4128, 5152, 6176, 7200
V_BADA, V_N1, V_N2, V_CMW, V_FCB, V_GNG = 0, 48, 56, 64, 88, 110


class Op:
    __slots__ = ("eng", "fn", "deps", "odeps", "needs_inc", "dma", "dsem", "dval", "ms", "seq", "phase", "cost", "lat", "fin", "tag")

    def __init__(self, eng, fn, dma):
        self.eng = eng
        self.fn = fn
        self.dma = dma
        self.deps = []
        self.odeps = []
        self.needs_inc = False
        self.dsem = None
        self.dval = 0
        self.ms = 0
        self.fin = 0.0


PRIO_W = 0.05
NB1 = 3
NHB = 2
ENG_GA = "dve"
ENG_CB = "dve"
ENG_X1 = "dve"
BANKS_P = [0, 1, 2, 3]
BANKS_G = [4, 5, 6]
DEF_COST = {"pe": 0.235, "act": 0.62, "dve": 0.62, "pool": 1.4}


class Sched:
    def __init__(self, nc, es, dummies=None):
        self.nc = nc
        self.engs = {"pe": nc.tensor, "act": nc.scalar, "dve": nc.vector,
                     "pool": nc.gpsimd, "sp": nc.sync}
        self.sem = {e: es.enter_context(nc.semaphore("s_" + e)) for e in self.engs}
        self.dsems = {q: [es.enter_context(nc.semaphore("d_%s%d" % (q, i))) for i in range(ND)]
                      for q in ("sp", "pool")}
        self.last_w = {}
        self.readers = {}
        self.phases = [[]]
        self.nseq = 0
        self.dummies = dummies
        self.reorder = True
        self.prio_w = PRIO_W

    def add(self, eng, fn, reads=(), writes=(), dma=False, cost=None):
        op = Op(eng, fn, dma)
        deps = {}
        for t in reads:
            w = self.last_w.get(t)
            if w is not None:
                deps[id(w)] = w
            if isinstance(t, str) and t.startswith("ps"):
                for r in self.readers.get(t, ()):
                    if r.eng != eng:
                        deps[id(r)] = r
        for t in writes:
            w = self.last_w.get(t)
            if w is not None:
                deps[id(w)] = w
            for r in self.readers.get(t, ()):
                deps[id(r)] = r
        ph = len(self.phases) - 1
        op.phase = ph
        for d in deps.values():
            if d.phase != ph:
                continue
            op.odeps.append(d)
            if d.eng == "pe" and eng == "pe" and not d.dma and not dma:
                continue
            op.deps.append(d)
        for t in reads:
            self.readers.setdefault(t, []).append(op)
        for t in writes:
            self.last_w[t] = op
            self.readers[t] = []
        op.seq = self.nseq
        op.tag = getattr(self, "tag", "")
        self.nseq += 1
        if dma:
            op.cost = cost if cost is not None else 1.0
            op.lat = op.cost + 2.0
        else:
            op.cost = cost if cost is not None else DEF_COST[eng]
            op.lat = op.cost
        self.phases[-1].append(op)
        return op

    addb = add

    def barrier(self, dummies=None):
        self.phases.append([])

    def _schedule(self, ops):
        import heapq
        order = {e: [] for e in self.engs}
        if not self.reorder:
            for op in ops:
                order[op.eng].append(op)
            return order
        users = {}
        indeg = {}
        for op in ops:
            indeg[id(op)] = len(op.odeps)
            for d in op.odeps:
                users.setdefault(id(d), []).append(op)
        free = {e: 0.0 for e in self.engs}

        free["dmabw"] = 0.0

        def est(op):
            t = free[op.eng]
            if op.dma and free["dmabw"] > t:
                t = free["dmabw"]
            for d in op.odeps:
                x = d.fin + (0.45 if d.eng != op.eng else 0.1)
                if x > t:
                    t = x
            return t

        tail = {}
        for op in reversed(ops):
            m = 0.0
            for u in users.get(id(op), ()):
                x = tail[id(u)]
                if x > m:
                    m = x
            tail[id(op)] = m + op.lat
        PW = self.prio_w

        def key_of(op, t):
            return t - PW * tail[id(op)]

        heap = []
        for op in ops:
            if indeg[id(op)] == 0:
                heapq.heappush(heap, (key_of(op, est(op)), op.seq, op))
        n = 0
        while heap:
            key, _, op = heapq.heappop(heap)
            t = est(op)
            k2 = key_of(op, t)
            if k2 > key + 1e-9:
                heapq.heappush(heap, (k2, op.seq, op))
                continue
            if op.dma:
                free[op.eng] = t + (1.0 if op.eng == "pool" else 0.15)
                free["dmabw"] = t + op.cost
            else:
                free[op.eng] = t + op.cost
            op.fin = t + op.lat
            order[op.eng].append(op)
            n += 1
            for u in users.get(id(op), ()):
                indeg[id(u)] -= 1
                if indeg[id(u)] == 0:
                    heapq.heappush(heap, (key_of(u, est(u)), u.seq, u))
        assert n == len(ops), (n, len(ops))
        if getattr(self, "report", False):
            busy = {e: sum(o.cost for o in order[e] if not o.dma) for e in order}
            busy["dma"] = sum(o.cost for e in order for o in order[e] if o.dma)
            print("phase busy", {e: round(v) for e, v in busy.items()}, "span", round(max(free.values())))
            tags = {}
            for o in ops:
                a = tags.setdefault(o.tag, [1e9, 0.0, 0])
                a[0] = min(a[0], o.fin - o.lat)
                a[1] = max(a[1], o.fin)
                a[2] += 1
            for k, v in sorted(tags.items(), key=lambda kv: kv[1][0]):
                print("   tag %-12s start %7.1f end %7.1f n=%d" % (k, v[0], v[1], v[2]))
        self.est_time = getattr(self, "est_time", []) + [max(o.fin for o in ops) if ops else 0.0]
        return order

    def emit(self, block):
        dm = self.dummies
        final = {e: [] for e in self.engs}
        pend = {"pe": [], "sp": []}
        nph = len(self.phases)
        for pi, ops in enumerate(self.phases):
            order = self._schedule(ops)
            for e in self.engs:
                lst = order[e]
                if e in pend and pend[e] and lst:
                    lst[0].deps = list(lst[0].deps) + pend[e]
                    pend[e] = []
                final[e].extend(lst)
            if pi < nph - 1:
                firsts = []
                for e in ("act", "dve", "pool"):
                    t = dm[e]
                    o = Op(e, (lambda E, t=t, e=e: (E.memzero(t) if e == "act" else E.memset(t, 0.0))), False)
                    o.phase = -1
                    if e == "pool":
                        o.deps = [x for x in ops if x.dma]
                    firsts.append(o)
                    final[e].append(o)
                seconds = []
                lastpe = order["pe"][-1] if order["pe"] else None
                for e in ("act", "dve", "pool"):
                    t = dm[e]
                    o = Op(e, (lambda E, t=t, e=e: (E.memzero(t) if e == "act" else E.memset(t, 0.0))), False)
                    o.phase = -1
                    o.deps = list(firsts) + ([lastpe] if lastpe is not None else [])
                    seconds.append(o)
                    final[e].append(o)
                pend = {"pe": list(seconds), "sp": list(seconds)}
        dcnt = {q: [0] * ND for q in self.dsems}
        for q in self.dsems:
            rr = 0
            last = [None] * ND
            for op in final[q]:
                if not op.dma:
                    continue
                s = rr % ND
                rr += 1
                if last[s] is not None:
                    op.deps = list(op.deps) + [last[s]]
                dcnt[q][s] += 1
                op.dsem = self.dsems[q][s]
                op.dval = 16 * dcnt[q][s]
                last[s] = op
        for e in self.engs:
            for op in final[e]:
                for d in op.deps:
                    if not d.dma:
                        d.needs_inc = True
        for e, lst in final.items():
            c = 0
            for op in lst:
                if op.needs_inc and not op.dma:
                    c += 1
                    op.ms = c
        stats = {}
        names = {"pe": "tensor", "act": "scalar", "dve": "vector", "pool": "gpsimd", "sp": "sync"}
        for e in self.engs:
            lst = final[e]
            E_sem = self.sem[e]

            def run(E, lst=lst, e=e, E_sem=E_sem):
                seen = {}
                nw = 0
                for op in lst:
                    for d in op.deps:
                        if d.dma:
                            key = ("d", id(d.dsem))
                            sem, val = d.dsem, d.dval
                        else:
                            key = ("c", d.eng)
                            sem, val = self.sem[d.eng], d.ms
                        if seen.get(key, 0) >= val:
                            continue
                        seen[key] = val
                        E.wait_ge(sem, val)
                        nw += 1
                    ins = op.fn(E)
                    if op.dma:
                        ins.then_inc(op.dsem, 16)
                    elif op.needs_inc:
                        ins.then_inc(E_sem, 1)
                if e == "sp":
                    for q in self.dsems:
                        for s in range(ND):
                            if dcnt[q][s]:
                                E.wait_ge(self.dsems[q][s], 16 * dcnt[q][s])
                stats[e] = (len(lst), nw)

            getattr(block, names[e])(run)
        stats["est_us"] = [round(x) for x in getattr(self, "est_time", [])]
        return stats


def build_program(debug=False, stop_after=99, reorder=True):
    nc = bass.Bass("TRN2", target_bir_lowering=False)

    def din(name, shape, dt=F32):
        return nc.dram_tensor(name, shape, dt, kind="ExternalInput").ap()

    def dout(name, shape, dt=F32):
        return nc.dram_tensor(name, shape, dt, kind="ExternalOutput").ap()

    def dscr(name, shape, dt):
        if debug:
            return nc.dram_tensor(name, shape, dt, kind="ExternalOutput").ap()
        return nc.dram_tensor(name, shape, dt).ap()

    xin = din("xin", [NTOK, 1024])
    cv2 = din("cv2", [2, 1024])
    s0f = din("s0f", [128, 1024])
    s0b = din("s0b", [128, 1024])
    lnk = din("lnk", [128, 4])
    msk = din("msk", [2, NTOK])
    flagA = din("flagA", [128, 9])
    vecs = din("vecs", [112, 128])
    cwr = din("cwr", [198, 128])
    w_ada = din("w_ada", [1024, 6144])
    b_ada = din("b_ada", [6144])
    w_in = din("w_in", [1024, 8224])
    wgk = din("wgk", [17, 2, 512])
    w_out = din("w_out", [1024, 1024])
    w_up = din("w_up", [1024, DFF])
    w_gate = din("w_gate", [1024, DFF])
    w_down = din("w_down", [DFF, 1024])
    normf = din("normf", [1024])
    y = dout("y", [NTOK, 1024])
    sfo = dout("sfo", [6, 128, 1024])
    sbo = dout("sbo", [6, 128, 1024])
    xnT_s = dscr("xnT_s", [128, 8, NTOK], BF16)
    xn2T_s = dscr("xn2T_s", [128, 8, NTOK], BF16)
    SbS_s = dscr("SbS_s", [NT, 128, 1024], BF16)
    x1_s = dscr("x1_s", [NTOK, 1024], F32)
    ga2_s = dscr("ga2_s", [128, 2, 1024], F32)
    kv_s = dscr("kv_s", [NT, 128, 1536], BF16)

    w_in_v = w_in.rearrange("(k p) n -> p k n", p=128)
    w_ada_v = w_ada.rearrange("(k p) n -> p k n", p=128)
    w_out_v = w_out.rearrange("(k p) n -> p k n", p=128)
    w_up_v = w_up.rearrange("(k p) n -> p k n", p=128)
    w_gate_v = w_gate.rearrange("(k p) n -> p k n", p=128)
    w_down_v = w_down.rearrange("(c p) n -> p c n", p=128)

    with ExitStack() as es:
        S = Sched(nc, es)

        def T(name, shape, dt, st=es):
            return st.enter_context(nc.sbuf_tensor(name, shape, dt))

        PS = [es.enter_context(nc.psum_tensor("ps%d" % i, [128, 512], F32)) for i in range(7)]
        psm = es.enter_context(nc.psum_tensor("psm", [128, 512], F32))
        PT = ["ps%d" % i for i in range(7)]

        ident = T("ident", [128, 128], F32)
        identb = T("identb", [128, 128], BF16)
        epsb = T("epsb", [128, 2], F32)
        vecT = T("vecT", [128, 112], F32)
        cwT = T("cwT", [128, 198], F32)
        cwA = T("cwA", [128, 198], F32)
        lnk_t = T("lnk_t", [128, 4], F32)
        flg_t = T("flg_t", [128, 9], F32)
        eff = T("eff", [128, 4, 8, 2], F32)
        gaBC = T("gaBC", [128, 1, 2, 1024], F32)
        dmy = T("dmy", [128, 8], F32)
        dummies = {"act": dmy[:, 0:1], "dve": dmy[:, 1:2], "pool": dmy[:, 2:3]}
        S.dummies = dummies
        S.reorder = reorder
        e12 = es.enter_context(ExitStack())
        maskF = T("maskF", [128, 4, 128], F32, e12)
        maskB = T("maskB", [128, 4, 128], F32, e12)
        Rf = T("Rf", [128, 2, 128], BF16, e12)
        Rb = T("Rb", [128, 2, 128], BF16, e12)
        T3f = T("T3f", [128, 128], BF16, e12)
        T3b = T("T3b", [128, 128], BF16, e12)
        ones_bf = T("ones_bf", [128, 128], BF16, e12)
        rcol = T("rcol", [128, 2, 2], BF16, e12)
        cfT = T("cfT", [17, NTOK], BF16, e12)
        cbT = T("cbT", [17, NTOK], BF16, e12)
        wgk_t = T("wgk_t", [17, 2, 512], BF16, e12)

        block = es.enter_context(nc.Block())
        A = S.addb

        def pool_mask(out_ap, tok, pattern, cm, base):
            A("pool", lambda E: E.memset(out_ap, 1.0), writes=[tok])
            A("pool", lambda E: E.affine_select(out=out_ap, in_=out_ap, pattern=pattern, compare_op=ALU.is_ge,
                                                fill=0.0, base=base, channel_multiplier=cm), reads=[tok], writes=[tok])

        e0 = ExitStack()
        if True:
            refF = T("refF", [128, 128], F32, e0)
            refB = T("refB", [128, 128], F32, e0)
            vrow = T("vrow", [112, 128], F32, e0)
            cwr0 = T("cwr0", [128, 128], F32, e0)
            cwr1 = T("cwr1", [70, 128], F32, e0)
            cT = T("cT", [128, 8, 2], F32, e0)
            cTb = T("cTb", [128, 8, 2], BF16, e0)
            cBC = T("cBC", [128, 8, 2, 128], BF16, e0)
            wab = [T("wab%d" % i, [128, 8, 1024], BF16, e0) for i in range(2)]
            bbc = T("bbc", [128, 2, 1024], F32, e0)
            modT = T("modT", [128, 4, 8, 2], F32, e0)
            ga2t = T("ga2t", [128, 1, 2, 1024], F32, e0)

            A("pool", lambda E: E.memset(ident[:], 1.0), writes=["ident"])
            A("pool", lambda E: E.affine_select(out=ident[:], in_=ident[:], pattern=[[-1, 128]], compare_op=ALU.is_equal,
                                                fill=0.0, base=0, channel_multiplier=1), reads=["ident"], writes=["ident"])
            A("dve", lambda E: E.tensor_copy(out=identb[:], in_=ident[:]), reads=["ident"], writes=["identb"])
            pool_mask(maskF[:], "maskF", [[0, 4], [1, 128]], -1, 0)
            pool_mask(maskB[:], "maskB", [[0, 4], [-1, 128]], 1, 0)
            pool_mask(refF[:], "refF", [[0, 128]], -1, 64)
            pool_mask(refB[:], "refB", [[0, 128]], 1, -63)
            A("pool", lambda E: E.memset(ones_bf[:], 1.0), writes=["ones_bf"])
            A("pool", lambda E: E.memset(rcol[:], 1.0), writes=["rcol"])
            A("dve", lambda E: E.tensor_copy(out=rcol[:, 0, 1:2], in_=refF[:, 0:1]), reads=["refF", "rcol"], writes=["rcol"])
            A("dve", lambda E: E.tensor_copy(out=rcol[:, 1, 1:2], in_=refB[:, 0:1]), reads=["refB", "rcol"], writes=["rcol"])
            A("pool", lambda E: E.memset(epsb[:, 0:1], EPS), writes=["epsb0"])
            A("pool", lambda E: E.memset(epsb[:, 1:2], EPS * 128.0), writes=["epsb1"])
            A("pool", lambda E: E.memset(cfT[:], 1.0), writes=["cfT"])
            A("pool", lambda E: E.memset(cbT[:], 1.0), writes=["cbT"])
            A("dve", lambda E: E.tensor_tensor(out=Rf[:, 0, :], in0=maskF[:, 0, :], in1=refF[:], op=ALU.subtract), reads=["maskF", "refF"], writes=["Rf"])
            A("dve", lambda E: E.tensor_copy(out=Rf[:, 1, :], in_=maskF[:, 0, :]), reads=["maskF"], writes=["Rf"])
            A("dve", lambda E: E.tensor_tensor(out=Rb[:, 0, :], in0=maskB[:, 0, :], in1=refB[:], op=ALU.subtract), reads=["maskB", "refB"], writes=["Rb"])
            A("dve", lambda E: E.tensor_copy(out=Rb[:, 1, :], in_=maskB[:, 0, :]), reads=["maskB"], writes=["Rb"])
            A("dve", lambda E: E.tensor_scalar(out=T3f[:], in0=maskF[:, 0, :], scalar1=-1.0, scalar2=1.0, op0=ALU.mult, op1=ALU.add), reads=["maskF"], writes=["T3f"])
            A("dve", lambda E: E.tensor_scalar(out=T3b[:], in0=maskB[:, 0, :], scalar1=-1.0, scalar2=1.0, op0=ALU.mult, op1=ALU.add), reads=["maskB"], writes=["T3b"])
            A("sp", lambda E: E.dma_start(out=vrow[:], in_=vecs), writes=["vrow"], dma=True)
            A("sp", lambda E: E.dma_start(out=cwr0[:], in_=cwr[0:128, :]), writes=["cwr0"], dma=True)
            A("sp", lambda E: E.dma_start(out=cwr1[:], in_=cwr[128:198, :]), writes=["cwr1"], dma=True)
            A("sp", lambda E: E.dma_start(out=lnk_t[:], in_=lnk), writes=["lnk"], dma=True)
            A("sp", lambda E: E.dma_start(out=flg_t[:], in_=flagA), writes=["flg"], dma=True)
            for c_ in range(2):
                A("sp", lambda E, c_=c_: E.dma_start(out=cT[:, :, c_], in_=cv2[c_].rearrange("(k p) -> p k", p=128), allow_slow_non_contiguous=True), writes=["cT"], dma=True)
            A("pool", lambda E: E.dma_start(out=wgk_t[:], in_=wgk), writes=["wgk"], dma=True)
            A("pe", lambda E: E.transpose(out=PS[0][:, 0:112], in_=vrow[:], identity=ident[0:112, 0:112]), reads=["vrow", "ident"], writes=[PT[0]])
            A("dve", lambda E: E.tensor_copy(out=vecT[:], in_=PS[0][:, 0:112]), reads=[PT[0]], writes=["vecT"])
            A("pe", lambda E: E.transpose(out=PS[1][:, 0:128], in_=cwr0[:], identity=ident[:]), reads=["cwr0", "ident"], writes=[PT[1]])
            A("pe", lambda E: E.transpose(out=PS[1][:, 128:198], in_=cwr1[:], identity=ident[0:70, 0:70]), reads=["cwr1", "ident"], writes=[PT[1]])
            A("dve", lambda E: E.tensor_copy(out=cwT[:], in_=PS[1][:, 0:198]), reads=[PT[1]], writes=["cwT"])
            A("dve", lambda E: E.tensor_tensor(out=cwA[:].rearrange("p (t c) -> p t c", c=NFC), in0=cwT[:].rearrange("p (t c) -> p t c", c=NFC),
                                               in1=flg_t[:].unsqueeze(2).to_broadcast([128, 9, NFC]), op=ALU.mult), reads=["cwT", "flg"], writes=["cwA"])
            A("act", lambda E: E.activation(out=cT[:], in_=cT[:], func=AF.Silu), reads=["cT"], writes=["cT"])
            A("dve", lambda E: E.tensor_copy(out=cTb[:], in_=cT[:]), reads=["cT"], writes=["cTb"])
            A("dve", lambda E: E.tensor_copy(out=cBC[:], in_=cTb[:].unsqueeze(3).to_broadcast([128, 8, 2, 128])), reads=["cTb"], writes=["cBC"])
            order = [(1, "pp", 0), (0, "pp", 1), (4, "pp", 2), (3, "pp", 3), (2, "bc", 0), (5, "bc", 1)]

            def mod_block(n, extra_reads=()):
                v, kind, slot = order[n]
                wb = wab[n % 2]
                wt = "wab%d" % (n % 2)
                A("pool", lambda E, wb=wb, v=v: E.dma_start(out=wb[:], in_=w_ada_v[:, :, v * 1024:(v + 1) * 1024]), reads=list(extra_reads), writes=[wt], dma=True, cost=12.0)
                if kind == "pp":
                    pst = PS[2 + (n % 2)]
                    ptk = PT[2 + (n % 2)]
                    for j in range(8):
                        for k in range(8):
                            A("pe", lambda E, wb=wb, j=j, k=k, pst=pst: E.matmul(pst[:, j * 2:j * 2 + 2], lhsT=wb[:, k, j * 128:(j + 1) * 128], rhs=cTb[:, k, :],
                                                                               start=(k == 0), stop=(k == 7)), reads=[wt, "cTb"], writes=[ptk], cost=mmc(2))
                    A("dve", lambda E, pst=pst, slot=slot, v=v: E.tensor_tensor(out=modT[:, slot, :, :], in0=pst[:, 0:16].rearrange("p (j c) -> p j c", c=2),
                                                                               in1=vecT[:, V_BADA + v * 8:V_BADA + v * 8 + 8].unsqueeze(2).to_broadcast([128, 8, 2]), op=ALU.add),
                      reads=[ptk, "vecT"], writes=["modT%d" % slot], cost=0.2)
                else:
                    A("sp", lambda E, slot=slot, v=v: E.dma_start(out=bbc[:, slot, :], in_=b_ada[v * 1024:(v + 1) * 1024].partition_broadcast(128)), reads=list(extra_reads), writes=["bbc%d" % slot], dma=True)
                    for cv in range(2):
                        for hf in range(2):
                            pst = PS[4 + hf]
                            ptk = PT[4 + hf]
                            for k in range(8):
                                A("pe", lambda E, wb=wb, k=k, cv=cv, hf=hf, pst=pst: E.matmul(pst[:], lhsT=cBC[:, k, cv, :], rhs=wb[:, k, hf * 512:(hf + 1) * 512],
                                                                                           start=(k == 0), stop=(k == 7)), reads=[wt, "cBC"], writes=[ptk])
                            gdst = gaBC if slot == 0 else ga2t
                            A("dve", lambda E, slot=slot, cv=cv, hf=hf, pst=pst, gdst=gdst: E.tensor_tensor(out=gdst[:, 0, cv, hf * 512:(hf + 1) * 512], in0=pst[:],
                                                                                              in1=bbc[:, slot, hf * 512:(hf + 1) * 512], op=ALU.add),
                              reads=[ptk, "bbc%d" % slot], writes=["gaBC%d" % slot])
                    if slot == 1:
                        A("sp", lambda E: E.dma_start(out=ga2_s, in_=ga2t[:, 0, :, :]), reads=["gaBC1"], writes=["ga2_s"], dma=True)

            def mod_eff(i):
                sc_slot, sh_slot, goff = [(0, 1, V_N1), (2, 3, V_N2)][i]
                A("dve", lambda E: E.scalar_tensor_tensor(out=eff[:, 2 * i, :, :], in0=modT[:, sc_slot, :, :], scalar=1.0,
                                                          in1=vecT[:, goff:goff + 8].unsqueeze(2).to_broadcast([128, 8, 2]),
                                                          op0=ALU.add, op1=ALU.mult), reads=["modT%d" % sc_slot, "vecT"], writes=[("eff", i)], cost=0.2)
                A("dve", lambda E: E.tensor_copy(out=eff[:, 2 * i + 1, :, :], in_=modT[:, sh_slot, :, :]), reads=["modT%d" % sh_slot], writes=[("eff", i)], cost=0.2)

            mod_block(0)
            mod_block(1)
            mod_eff(0)

        def norm_tile(xt_ap, xt_tok, which, cv, dst, dst_tok, scr, p, banks=(0, 1), all_act=False):
            ss, rs, xs = scr
            xs_tok = "xs_" + xs.name if hasattr(xs, "name") else "xs"
            A("act", lambda E: E.activation(out=xs[:], in_=xt_ap, func=AF.Square, accum_out=ss[:, p:p + 1]), reads=[xt_tok], writes=[xs_tok, "ss%d" % p], cost=0.85)
            A("act", lambda E: E.activation(out=rs[:, p:p + 1], in_=ss[:, p:p + 1], func=AF.Ln, scale=1.0 / 1024, bias=epsb[:, 0:1]), reads=["ss%d" % p, "epsb0"], writes=["rs%d" % p], cost=0.2)
            A("act", lambda E: E.activation(out=rs[:, p:p + 1], in_=rs[:, p:p + 1], func=AF.Exp, scale=-0.5), reads=["rs%d" % p], writes=["rs%d" % p], cost=0.2)
            A("dve", lambda E: E.tensor_scalar(out=xs[:], in0=xt_ap, scalar1=rs[:, p:p + 1], scalar2=None, op0=ALU.mult), reads=[xt_tok, "rs%d" % p], writes=[xs_tok], cost=1.1)
            for k in range(8):
                b = banks[k // 4]
                A("pe", lambda E, k=k, b=b: E.transpose(out=PS[b][:, (k % 4) * 128:(k % 4 + 1) * 128], in_=xs[:, k * 128:(k + 1) * 128], identity=ident[:]),
                  reads=[xs_tok, "ident"], writes=[PT[b]], cost=0.113)
            for k in range(8):
                b = banks[k // 4]
                src = PS[b][:, (k % 4) * 128:(k % 4 + 1) * 128]
                sc = eff[:, 2 * which, k, cv:cv + 1]
                sh = eff[:, 2 * which + 1, k, cv:cv + 1]
                if k < 4 and not all_act:
                    A("dve", lambda E, k=k, src=src, sc=sc, sh=sh: E.tensor_scalar(out=dst[:, k, :], in0=src, scalar1=sc, scalar2=sh, op0=ALU.mult, op1=ALU.add),
                      reads=[PT[b], ("eff", which)], writes=[(dst_tok, k)], cost=0.3)
                else:
                    A("act", lambda E, k=k, src=src, sc=sc, sh=sh: E.activation(out=dst[:, k, :], in_=src, func=AF.Identity, scale=sc, bias=sh),
                      reads=[PT[b], ("eff", which)], writes=[(dst_tok, k)], cost=0.35)

        with ExitStack() as e1:
            wk = T("wk", [128, 8, 512], BF16, e1)
            wv = T("wv", [128, 8, 1024], BF16, e1)
            wcd = T("wcd", [128, 8, 32], BF16, e1)
            xt = [T("p1xt%d" % i, [128, 1024], F32, e1) for i in range(NB1)]
            xsb = [T("p1xs%d" % i, [128, 1024], F32, e1) for i in range(NB1)]
            ss = T("ss", [128, 4], F32, e1)
            rs = T("rs", [128, 4], F32, e1)
            xnt = [T("xnt%d" % i, [128, 8, 128], BF16, e1) for i in range(NB1)]
            kvb = [T("kvb%d" % i, [128, 1536], BF16, e1) for i in range(NB1)]
            kSb = [kvb[i][:, 0:512] for i in range(NB1)]
            vSb = [kvb[i][:, 512:1536] for i in range(NB1)]
            ex1b = [T("ex1_%d" % i, [128, 512], F32, e1) for i in range(NB1)]
            spbb = [T("spb%d" % i, [128, 512], BF16, e1) for i in range(NB1)]
            e3b_ = [T("e3_%d" % i, [128, 512], F32, e1) for i in range(NB1)]
            Khb = [T("Kh%d" % i, [128, 512], BF16, e1) for i in range(NB1)]
            decb = [T("dec%d" % i, [128, 8], F32, e1) for i in range(NB1)]
            ones_c = T("ones_c", [128, 1], BF16, e1)
            Sb = [T("Sb%d" % i, [128, 1024], F32, e1) for i in range(2)]
            sbs = [T("sbs%d" % i, [128, 1024], BF16, e1) for i in range(NB1)]

            A("pool", lambda E: E.memset(ones_c[:], 1.0), writes=["ones_c"])
            A("pool", lambda E: E.dma_start(out=wk[:], in_=w_in_v[:, :, C_K:C_K + 512]), writes=["wk"], dma=True, cost=6.0)
            A("pool", lambda E: E.dma_start(out=wcd[:], in_=w_in_v[:, :, C_CF:C_CF + 32]), writes=["wcd"], dma=True)
            A("pool", lambda E: E.dma_start(out=wv[:], in_=w_in_v[:, :, C_V:C_V + 1024]), writes=["wv"], dma=True, cost=12.0)
            A("pool", lambda E: E.memset(Sb[1][:], 0.0), writes=[("Sb1", h_) for h_ in range(4)])
            bank1 = [0]

            def nb1():
                b_ = bank1[0] % 7
                bank1[0] += 1
                return b_

            cnt = 0
            for t in reversed(range(NT)):
                p = cnt % NB1
                cnt += 1
                slot = t // 2
                cv = 0 if t < 8 else 1
                tc_ = slice(t * 128, (t + 1) * 128)
                xs, kS, vS, ex1, spb, e3, Kh, dec = xsb[p], kSb[p], vSb[p], ex1b[p], spbb[p], e3b_[p], Khb[p], decb[p]
                sfx = "_%d" % p
                A("sp", lambda E, p=p, t=t: E.dma_start(out=xt[p][:], in_=xin[t * 128:(t + 1) * 128, :]), writes=["xt%d" % p], dma=True, cost=1.5)
                norm_tile(xt[p][:], "xt%d" % p, 0, cv, xnt[p], "xnt%d" % p, (ss, rs, xs), p, banks=(nb1(), nb1()))
                A("sp", lambda E, p=p, tc_=tc_: E.dma_start(out=xnT_s[:, :, tc_], in_=xnt[p][:]), reads=[("xnt%d" % p, k_) for k_ in range(8)], writes=[("xnT_s", t)], dma=True)
                if cnt in (3, 5, 7, 9):
                    nblk = 2 + (cnt - 3) // 2
                    mod_block(nblk, extra_reads=[("xnt%d" % p, 7)])
                    if nblk == 3:
                        mod_eff(1)
                X = xnt[p]
                xtok = "xnt%d" % p
                bk, bv0, bv1 = nb1(), nb1(), nb1()
                for k in range(8):
                    A("pe", lambda E, k=k, X=X, bk=bk: E.matmul(PS[bk][:], lhsT=X[:, k, :], rhs=wk[:, k, :], start=(k == 0), stop=(k == 7)), reads=[(xtok, k), "wk"], writes=[PT[bk]])
                for hf in range(2):
                    bv = (bv0, bv1)[hf]
                    for k in range(8):
                        A("pe", lambda E, k=k, X=X, hf=hf, bv=bv: E.matmul(PS[bv][:], lhsT=X[:, k, :], rhs=wv[:, k, hf * 512:(hf + 1) * 512], start=(k == 0), stop=(k == 7)),
                          reads=[(xtok, k), "wv"], writes=[PT[bv]])
                for c in range(2):
                    for k in range(8):
                        A("pe", lambda E, k=k, X=X, c=c, p=p: E.matmul(psm[0:16, (p % 2) * 256 + c * 128:(p % 2) * 256 + (c + 1) * 128], lhsT=wcd[:, k, c * 16:(c + 1) * 16], rhs=X[:, k, :], start=(k == 0), stop=(k == 7)),
                          reads=[(xtok, k), "wcd"], writes=["psm"], cost=0.1)
                A("act", lambda E, kS=kS, bk=bk: E.copy(out=kS, in_=PS[bk][:]), reads=[PT[bk]], writes=["kS" + sfx])
                A("act", lambda E, vS=vS, bv0=bv0: E.copy(out=vS[:, 0:512], in_=PS[bv0][:]), reads=[PT[bv0]], writes=["vS0" + sfx])
                A("dve", lambda E, vS=vS, bv1=bv1: E.tensor_copy(out=vS[:, 512:1024], in_=PS[bv1][:]), reads=[PT[bv1]], writes=["vS1" + sfx])
                A("dve", lambda E, tc_=tc_, p=p: E.tensor_copy(out=cfT[0:16, tc_], in_=psm[0:16, (p % 2) * 256:(p % 2) * 256 + 128]), reads=["psm", "cfT"], writes=[("cfT", t)], cost=0.2)
                A("dve", lambda E, tc_=tc_, p=p: E.tensor_copy(out=cbT[0:16, tc_], in_=psm[0:16, (p % 2) * 256 + 128:(p % 2) * 256 + 256]), reads=["psm", "cbT"], writes=[("cbT", t)], cost=0.2)
                bl = nb1()
                A("pe", lambda E, tc_=tc_, bl=bl: E.matmul(PS[bl][:], lhsT=cbT[0:17, tc_], rhs=wgk_t[:, 1, :], start=True, stop=True), reads=[("cbT", t), "cbT", "wgk"], writes=[PT[bl]])
                A("act", lambda E, ex1=ex1, bl=bl: E.activation(out=ex1[:], in_=PS[bl][:], func=AF.Exp, scale=-1.0), reads=[PT[bl]], writes=["ex1" + sfx])
                A("act", lambda E, ex1=ex1, spb=spb: E.activation(out=spb[:], in_=ex1[:], func=AF.Ln, bias=1.0), reads=["ex1" + sfx], writes=["spb" + sfx])
                be3, bd = nb1(), nb1()
                A("pe", lambda E, spb=spb, be3=be3: E.matmul(PS[be3][:], lhsT=T3b[:], rhs=spb[:], start=True, stop=True), reads=["T3b", "spb" + sfx], writes=[PT[be3]])
                for h in range(4):
                    A("pe", lambda E, h=h, spb=spb, bd=bd: E.matmul(PS[bd][:, 2 * h:2 * h + 2], lhsT=spb[:, h * 128:(h + 1) * 128], rhs=rcol[:, 1, :], start=True, stop=True),
                      reads=["spb" + sfx, "rcol"], writes=[PT[bd]], cost=0.1)
                A("act", lambda E, e3=e3, be3=be3: E.activation(out=e3[:], in_=PS[be3][:], func=AF.Exp, scale=-1.0 / 16), reads=[PT[be3]], writes=["e3" + sfx])
                A("act", lambda E, dec=dec, bd=bd: E.activation(out=dec[:], in_=PS[bd][:, 0:8], func=AF.Exp, scale=-1.0 / 16), reads=[PT[bd]], writes=["dec" + sfx], cost=0.3)
                A("dve", lambda E, Kh=Kh, kS=kS, e3=e3: E.tensor_tensor(out=Kh[:], in0=kS, in1=e3[:], op=ALU.mult), reads=["kS" + sfx, "e3" + sfx], writes=["Kh" + sfx])
                A("sp", lambda E, p=p, t=t: E.dma_start(out=kv_s[t], in_=kvb[p][:]), reads=["kS" + sfx, "vS0" + sfx, "vS1" + sfx], writes=[("kv_s", t)], dma=True)
                bu = (nb1(), nb1())
                for h in range(4):
                    b_ = bu[h // 2]
                    A("pe", lambda E, h=h, b_=b_, Kh=Kh, vS=vS: E.matmul(PS[b_][:, (h % 2) * 256:(h % 2 + 1) * 256], lhsT=Kh[:, h * 128:(h + 1) * 128], rhs=vS[:, h * 256:(h + 1) * 256],
                                                        start=True, stop=True), reads=["Kh" + sfx, "vS0" + sfx, "vS1" + sfx], writes=[PT[b_]], cost=0.15)
                cur = Sb[slot % 2]
                ctok = "Sb%d" % (slot % 2)
                for h in range(4):
                    A("act", lambda E, p=p, cur=cur, h=h, dec=dec: E.activation(out=sbs[p][:, h * 256:(h + 1) * 256], in_=cur[:, h * 256:(h + 1) * 256], func=AF.Identity,
                                                                               scale=dec[:, 2 * h + 1:2 * h + 2]), reads=[(ctok, h), "dec" + sfx], writes=[("sbs%d" % p, h)], cost=0.4)
                A("sp", lambda E, p=p, t=t: E.dma_start(out=SbS_s[t], in_=sbs[p][:]), reads=[("sbs%d" % p, h_) for h_ in range(4)], writes=[("SbS_s", t)], dma=True)
                for h in range(4):
                    b_ = bu[h // 2]
                    A("dve", lambda E, h=h, b_=b_, cur=cur, dec=dec: E.scalar_tensor_tensor(out=cur[:, h * 256:(h + 1) * 256], in0=cur[:, h * 256:(h + 1) * 256], scalar=dec[:, 2 * h:2 * h + 1],
                                                                                in1=PS[b_][:, (h % 2) * 256:(h % 2 + 1) * 256], op0=ALU.mult, op1=ALU.add),
                      reads=[(ctok, h), "dec" + sfx, PT[b_]], writes=[(ctok, h)], cost=0.45)
                if t % 2 == 0:
                    call = [(ctok, h_) for h_ in range(4)]
                    A("sp", lambda E, slot=slot, cur=cur: E.dma_start(out=sbo[slot], in_=cur[:]), reads=call, dma=True)
                    if slot > 0:
                        ns = slot - 1
                        nxt = Sb[ns % 2]
                        ntok = "Sb%d" % (ns % 2)
                        nall = [(ntok, h_) for h_ in range(4)]
                        if ns == 4:
                            A("pool", lambda E, nxt=nxt: E.memset(nxt[:], 0.0), writes=nall)
                        elif ns == 3:
                            A("sp", lambda E, nxt=nxt: E.dma_start(out=nxt[:], in_=s0b), writes=nall, dma=True)
                        else:
                            A("dve", lambda E, nxt=nxt, cur=cur: E.tensor_scalar(out=nxt[:], in0=cur[:], scalar1=lnk_t[:, 0:1], scalar2=None, op0=ALU.mult),
                              reads=call + ["lnk"], writes=nall)
            S.barrier(dummies)
        e0.close()

        if stop_after >= 2:
            _pass2(nc, S, A, T, PS, PT, psm, locals())
        e12.close()
        if stop_after >= 3:
            _ffn(nc, S, A, T, PS, PT, psm, locals())
        stats = S.emit(block)
    return nc, stats


def mmc(n):
    if n >= 512:
        return 0.235
    if n >= 256:
        return 0.19
    if n >= 128:
        return 0.113
    return 0.03


def _pass2(nc, S, A, T, PS, PT, psm, L):
    g_ = L
    xin, xnT_s, xn2T_s, SbS_s, x1_s, s0f, sfo, kv_s = g_["xin"], g_["xnT_s"], g_["xn2T_s"], g_["SbS_s"], g_["x1_s"], g_["s0f"], g_["sfo"], g_["kv_s"]
    w_in_v, w_out_v = g_["w_in_v"], g_["w_out_v"]
    ident, maskF, maskB, Rf, Rb, T3f = g_["ident"], g_["maskF"], g_["maskB"], g_["Rf"], g_["Rb"], g_["T3f"]
    rcol = g_["rcol"]
    ones_bf, epsb, vecT, lnk_t, eff, gaBC, cfT, cbT, wgk_t = g_["ones_bf"], g_["epsb"], g_["vecT"], g_["lnk_t"], g_["eff"], g_["gaBC"], g_["cfT"], g_["cbT"], g_["wgk_t"]
    norm_tile = g_["norm_tile"]
    with ExitStack() as e2:
        xg = T("xg", [128, 8, 512], BF16, e2)
        wb = [T("wb%d" % i, [128, 8, 512], BF16, e2) for i in range(3)]
        EE = [T("EE%d" % i, [128, 4, 512], BF16, e2) for i in range(2)]
        E2 = [T("E2%d" % i, [128, 4, 512], BF16, e2) for i in range(2)]
        decf = T("decf", [128, 4, 8], F32, e2)
        Qv = T("Qv", [128, 2, 4, 512], BF16, e2)
        Kv = T("Kv", [128, 2, 4, 512], BF16, e2)
        kvt = [T("kvt%d" % i, [128, 1536], BF16, e2) for i in range(2)]
        e3t = [T("e3t%d" % i, [128, 512], BF16, e2) for i in range(2)]
        Khf = [T("Khf%d" % i, [128, 512], BF16, e2) for i in range(2)]
        sp = [[T("sp0_%d" % i, [128, 512], BF16, e2) for i in range(4)], [T("sp1_%d" % i, [128, 512], BF16, e2) for i in range(2)]]
        AT = [[T("AT%d_%d" % (d, i), [128, 4, 128], BF16, e2) for i in range(2)] for d in range(2)]
        GA = T("GA", [128, 8, 512], BF16, e2)
        CB = T("CB", [128, 8, 512], BF16, e2)
        onS = T("onS", [128, 8, 512], BF16, e2)
        onfb = [T("onf%d" % i, [128, 8, 128], F32, e2) for i in range(2)]
        osqb = [T("osq%d" % i, [128, 8, 128], BF16, e2) for i in range(2)]
        rstb = [T("rst%d" % i, [128, 4, 128], F32, e2) for i in range(2)]
        tmpA = [T("tmpA%d" % i, [128, 512], F32, e2) for i in range(2)]
        cc = [T("cc%d" % i, [128, 512], F32, e2) for i in range(2)]
        z = T("z", [128, 4, 514], F32, e2)
        acc = T("acc", [128, 4, 512], F32, e2)
        zl = T("zl", [128, 8], F32, e2)
        ztmp = T("ztmp", [128, 8], F32, e2)
        xh = T("xh", [128, 8, 1], BF16, e2)
        ones_c = T("ones_c2", [128, 1], BF16, e2)
        Sf = [T("Sf%d" % i, [128, 1024], F32, e2) for i in range(2)]
        Sfb = [T("Sfb%d" % i, [128, 1024], BF16, e2) for i in range(2)]
        SbT = [T("SbT%d" % i, [128, 1024], BF16, e2) for i in range(2)]
        xtb = [T("xt2_%d" % i, [128, 1024], F32, e2) for i in range(2)]
        xs = T("xs2", [128, 1024], F32, e2)
        ss = T("ss2", [128, 2], F32, e2)
        rs = T("rs2", [128, 2], F32, e2)
        xn2 = [T("xn2_%d" % i, [128, 8, 128], BF16, e2) for i in range(2)]
        print("pass2 sbuf remaining", nc.sbuf_bytes_remaining)

        A("pool", lambda E: E.memset(ones_c[:], 1.0), writes=["ones_c2"])
        A("pool", lambda E: E.memset(z[:], 0.0), writes=[("z", j) for j in range(4)] + ["zh"])
        A("sp", lambda E: E.dma_start(out=Sf[0][:], in_=s0f), writes=[("Sf0", h_) for h_ in range(4)], dma=True)
        wcnt = [0]

        def load_w(src_v, c0):
            i = wcnt[0] % 3
            wcnt[0] += 1
            A("pool", lambda E, i=i: E.dma_start(out=wb[i][:], in_=src_v[:, :, c0:c0 + 512]), writes=["wb%d" % i], dma=True, cost=6.0)
            return wb[i], "wb%d" % i

        bank = {"p": 0, "g": 0}
        pools = {"p": BANKS_P, "g": BANKS_G}

        def nb(kind="g"):
            b = bank["g"] % 7
            bank["g"] += 1
            return b

        def proj_fm(wt, wtok, j, rhs, rtok):
            b = nb("p")
            for k in range(8):
                A("pe", lambda E, k=k, b=b: E.matmul(PS[b][:], lhsT=wt[:, k, j * 128:(j + 1) * 128], rhs=rhs[:, k, :], start=(k == 0), stop=(k == 7)),
                  reads=[wtok, rtok], writes=[PT[b]], cost=mmc(512))
            return b

        tac = [0]

        def ntmp():
            i = tac[0] % 2
            tac[0] += 1
            return tmpA[i], "tmpA%d" % i

        tcount = 0
        for g in range(3):
            S.tag = "g%d_gate" % g
            A("sp", lambda E, g=g: E.dma_start(out=xg[:], in_=xnT_s[:, :, g * 512:(g + 1) * 512]), writes=["xg"], dma=True, cost=3.0)
            for ti in range(4):
                t = 4 * g + ti
                p = t % 2
                tc_ = slice(t * 128, (t + 1) * 128)
                gc = slice(ti * 128, (ti + 1) * 128)
                for d in range(2):
                    cT_ = cfT if d == 0 else cbT
                    R = Rf if d == 0 else Rb
                    si = ti if d == 0 else p
                    spd = sp[d][si]
                    sptok = "sp%d_%d" % (d, si)
                    b = nb()
                    A("pe", lambda E, b=b, cT_=cT_, d=d, tc_=tc_: E.matmul(PS[b][:], lhsT=cT_[0:17, tc_], rhs=wgk_t[:, d, :], start=True, stop=True), writes=[PT[b]], cost=mmc(512))
                    A("act", lambda E, b=b: E.activation(out=PS[b][:], in_=PS[b][:], func=AF.Exp, scale=-1.0), reads=[PT[b]], writes=[PT[b]], cost=0.6)
                    A("act", lambda E, b=b, spd=spd: E.activation(out=spd[:], in_=PS[b][:], func=AF.Ln, bias=1.0), reads=[PT[b]], writes=[sptok], cost=0.6)
                    b = nb()
                    for h in range(4):
                        A("pe", lambda E, b=b, h=h, spd=spd, R=R: E.matmul(PS[b][:, h * 128:(h + 1) * 128], lhsT=spd[:, h * 128:(h + 1) * 128], rhs=R[:, 0, :], start=True, stop=True),
                          reads=[sptok], writes=[PT[b]], cost=mmc(128))
                    A("act", lambda E, b=b, d=d, gc=gc: E.activation(out=EE[d][:, :, gc], in_=PS[b][:].rearrange("p (h i) -> p h i", h=4), func=AF.Exp, scale=-1.0 / 16),
                      reads=[PT[b]], writes=[("EE", d, ti)], cost=0.6)
                    A("act", lambda E, b=b, d=d, gc=gc: E.activation(out=E2[d][:, :, gc], in_=PS[b][:].rearrange("p (h i) -> p h i", h=4), func=AF.Exp, scale=1.0 / 16),
                      reads=[PT[b]], writes=[("E2", d, ti)], cost=0.6)
            S.tag = "g%d_qk" % g
            wq, wqt = load_w(w_in_v, C_Q)
            for h in range(4):
                b = proj_fm(wq, wqt, h, xg, "xg")
                for d in range(2):
                    A("dve", lambda E, b=b, d=d, h=h: E.tensor_tensor(out=Qv[:, d, h, :], in0=PS[b][:], in1=EE[d][:, h, :], op=ALU.mult),
                      reads=[PT[b]] + [("EE", d, i) for i in range(4)], writes=[("Qv", d, h)])
            wkk, wkt = load_w(w_in_v, C_K)
            for h in range(4):
                b = proj_fm(wkk, wkt, h, xg, "xg")
                for d in range(2):
                    A("dve", lambda E, b=b, d=d, h=h: E.tensor_tensor(out=Kv[:, d, h, :], in0=PS[b][:], in1=E2[d][:, h, :], op=ALU.mult),
                      reads=[PT[b]] + [("E2", d, i) for i in range(4)], writes=[("Kv", d, h)])
            S.tag = "g%d_ogate" % g
            for hf in range(2):
                wG, wGt = load_w(w_in_v, C_G + hf * 512)
                wA, wAt = load_w(w_in_v, C_GA + hf * 512)
                tas = []
                for j in range(4):
                    b = proj_fm(wG, wGt, j, xg, "xg")
                    ta, tat = ntmp()
                    A("act", lambda E, b=b, ta=ta: E.activation(out=ta[:], in_=PS[b][:], func=AF.Silu), reads=[PT[b]], writes=[tat], cost=0.6)
                    b = proj_fm(wA, wAt, j, xg, "xg")
                    A("act", lambda E, b=b, j=j, hf=hf: E.activation(out=GA[:, hf * 4 + j, :], in_=PS[b][:], func=AF.Sigmoid), reads=[PT[b]], writes=[("GA", hf * 4 + j)], cost=0.6)
                    A(ENG_GA, lambda E, ta=ta, j=j, hf=hf: E.tensor_tensor(out=GA[:, hf * 4 + j, :], in0=GA[:, hf * 4 + j, :], in1=ta[:], op=ALU.mult),
                      reads=[tat, ("GA", hf * 4 + j)], writes=[("GA", hf * 4 + j)], cost=(1.3 if ENG_GA == "pool" else 0.6))
            S.tag = "g%d_conv" % g
            for hf in range(2):
                wcc, wcct = load_w(w_in_v, C_C + hf * 512)
                wcx, wcxt = load_w(w_in_v, C_X + hf * 512)
                for j in range(4):
                    b = proj_fm(wcc, wcct, j, xg, "xg")
                    A("act", lambda E, b=b, j=j: E.copy(out=cc[j % 2][:], in_=PS[b][:]), reads=[PT[b]], writes=["cc%d" % (j % 2)], cost=0.6)
                    b = proj_fm(wcx, wcxt, j, xg, "xg")
                    A("dve", lambda E, b=b, j=j: E.tensor_tensor(out=z[:, j, 1:513], in0=PS[b][:], in1=cc[j % 2][:], op=ALU.mult), reads=[PT[b], "cc%d" % (j % 2)], writes=[("z", j)])
                hs = slice(hf * 4, hf * 4 + 4)
                zall = [("z", j) for j in range(4)]
                if g == 0:
                    A("sp", lambda E: E.dma_start(out=xh[:], in_=xnT_s[:, :, 512:513], allow_slow_non_contiguous=True), writes=["xh"], dma=True)
                    for wi, (wt, wtok) in enumerate([(wcc, wcct), (wcx, wcxt)]):
                        for j in range(4):
                            for k in range(8):
                                A("pe", lambda E, wt=wt, wi=wi, j=j, k=k: E.matmul(psm[:, 300 + wi * 4 + j:301 + wi * 4 + j], lhsT=wt[:, k, j * 128:(j + 1) * 128], rhs=xh[:, k, :],
                                                                                 start=(k == 0), stop=(k == 7)), reads=[wtok, "xh"], writes=["psm"], cost=mmc(1))
                    A("dve", lambda E: E.tensor_copy(out=ztmp[:, 0:4], in_=psm[:, 300:304]), reads=["psm"], writes=["ztmp"], cost=0.2)
                    A("dve", lambda E: E.tensor_tensor(out=ztmp[:, 0:4], in0=ztmp[:, 0:4], in1=psm[:, 304:308], op=ALU.mult), reads=["psm", "ztmp"], writes=["ztmp"], cost=0.2)
                    A("dve", lambda E: E.tensor_scalar(out=z[:, :, 513], in0=ztmp[:, 0:4], scalar1=lnk_t[:, 0:1], scalar2=None, op0=ALU.mult), reads=["ztmp"] + zall, writes=["zh"] + zall, cost=0.2)
                    A("dve", lambda E, hs=hs: E.tensor_copy(out=zl[:, hs], in_=z[:, :, 512]), reads=zall, writes=["zl"], cost=0.2)
                elif g == 1:
                    A("dve", lambda E, hs=hs: E.tensor_scalar(out=z[:, :, 0], in0=zl[:, hs], scalar1=lnk_t[:, 0:1], scalar2=None, op0=ALU.mult), reads=["zl"] + zall, writes=["zh"] + zall, cost=0.2)
                    A("pool", lambda E: E.memset(z[:, :, 513], 0.0), reads=zall, writes=["zh"] + zall, cost=0.2)
                else:
                    A("pool", lambda E: E.memset(z[:, :, 0], 0.0), reads=zall, writes=["zh"] + zall, cost=0.2)
                w0 = vecT[:, V_CMW + 0 + hf * 4:V_CMW + 0 + hf * 4 + 4]
                w2 = vecT[:, V_CMW + 16 + hf * 4:V_CMW + 16 + hf * 4 + 4]
                for j in range(4):
                    c = hf * 4 + j
                    A("act", lambda E, j=j, c=c: E.activation(out=acc[:, j, :], in_=z[:, j, 1:513], func=AF.Identity, scale=vecT[:, V_CMW + 8 + c:V_CMW + 9 + c]),
                      reads=[("z", j)], writes=[("acc", j)], cost=0.7)
                    A("dve", lambda E, j=j, c=c: E.scalar_tensor_tensor(out=acc[:, j, :], in0=z[:, j, 0:512], scalar=vecT[:, V_CMW + c:V_CMW + c + 1], in1=acc[:, j, :],
                                                                       op0=ALU.mult, op1=ALU.add), reads=[("z", j), ("acc", j)], writes=[("acc", j)])
                    A("dve", lambda E, j=j, c=c: E.scalar_tensor_tensor(out=acc[:, j, :], in0=z[:, j, 2:514], scalar=vecT[:, V_CMW + 16 + c:V_CMW + 17 + c], in1=acc[:, j, :],
                                                                       op0=ALU.mult, op1=ALU.add), reads=[("z", j), ("acc", j)], writes=[("acc", j)])
                ncol = 1 if g < 2 else 2
                aall = [("acc", j) for j in range(4)]
                A("dve", lambda E, w0=w0: E.tensor_tensor(out=ztmp[:, 0:4], in0=z[:, :, 256], in1=w0, op=ALU.mult), reads=zall + ["ztmp"], writes=["ztmp"], cost=0.2)
                A("dve", lambda E, ncol=ncol: E.scalar_tensor_tensor(out=acc[:, :, 256], in0=ztmp[:, 0:4], scalar=lnk_t[:, ncol:ncol + 1], in1=acc[:, :, 256], op0=ALU.mult, op1=ALU.add),
                  reads=["ztmp"] + aall, writes=aall, cost=0.2)
                A("dve", lambda E, w2=w2: E.tensor_tensor(out=ztmp[:, 4:8], in0=z[:, :, 257], in1=w2, op=ALU.mult), reads=zall + ["ztmp"], writes=["ztmp"], cost=0.2)
                A("dve", lambda E, ncol=ncol: E.scalar_tensor_tensor(out=acc[:, :, 255], in0=ztmp[:, 4:8], scalar=lnk_t[:, ncol:ncol + 1], in1=acc[:, :, 255], op0=ALU.mult, op1=ALU.add),
                  reads=["ztmp"] + aall, writes=aall, cost=0.2)
                wcb, wcbt = load_w(w_in_v, C_B + hf * 512)
                for j in range(4):
                    b = proj_fm(wcb, wcbt, j, xg, "xg")
                    A("dve", lambda E, b=b, j=j: E.tensor_tensor(out=acc[:, j, :], in0=PS[b][:], in1=acc[:, j, :], op=ALU.mult), reads=[PT[b], ("acc", j)], writes=[("acc", j)])
                wgb, wgbt = load_w(w_in_v, C_GB + hf * 512)
                for j in range(4):
                    c = hf * 4 + j
                    b = proj_fm(wgb, wgbt, j, xg, "xg")
                    ta, tat = ntmp()
                    A("act", lambda E, b=b, ta=ta: E.activation(out=ta[:], in_=PS[b][:], func=AF.Sigmoid), reads=[PT[b]], writes=[tat], cost=0.6)
                    A(ENG_CB, lambda E, j=j, c=c, ta=ta: E.tensor_tensor(out=CB[:, c, :], in0=acc[:, j, :], in1=ta[:], op=ALU.mult), reads=[("acc", j), tat], writes=[("CB", c)], cost=(1.3 if ENG_CB == "pool" else 0.6))
            S.tag = "g%d_gla" % g
            GAall = [("GA", c) for c in range(8)]
            CBall = [("CB", c) for c in range(8)]
            for ti in range(4):
                t = 4 * g + ti
                slot = t // 2
                p = t % 2
                tcount += 1
                gc = slice(ti * 128, (ti + 1) * 128)
                cur = Sf[slot % 2]
                ctok = "Sf%d" % (slot % 2)
                vS = kvt[p][:, 512:1536]
                onf, osq, rst = onfb[p], osqb[p], rstb[p]
                onft, osqt, rstt = "onf%d" % p, "osq%d" % p, "rst%d" % p
                Qall = [("Qv", vi, h) for vi in range(4) for h in range(4)]
                Kall = [("Kv", d, h) for d in range(2) for h in range(4)]
                A("sp", lambda E, p=p, t=t: E.dma_start(out=SbT[p][:], in_=SbS_s[t]), writes=["SbT%d" % p], dma=True, cost=1.0)
                A("sp", lambda E, p=p, t=t: E.dma_start(out=kvt[p][:], in_=kv_s[t]), writes=["kvt%d" % p], dma=True, cost=1.2)
                spd = sp[0][ti]
                sptok = "sp0_%d" % ti
                b = nb()
                A("pe", lambda E, b=b, spd=spd: E.matmul(PS[b][:], lhsT=T3f[:], rhs=spd[:], start=True, stop=True), reads=[sptok], writes=[PT[b]], cost=mmc(512))
                A("act", lambda E, b=b, p=p: E.activation(out=e3t[p][:], in_=PS[b][:], func=AF.Exp, scale=-1.0 / 16), reads=[PT[b]], writes=["e3t%d" % p], cost=0.6)
                A("dve", lambda E, p=p: E.tensor_tensor(out=Khf[p][:], in0=kvt[p][:, 0:512], in1=e3t[p][:], op=ALU.mult), reads=["kvt%d" % p, "e3t%d" % p], writes=["Khf%d" % p], cost=0.45)
                bd = nb()
                for h in range(4):
                    A("pe", lambda E, h=h, bd=bd, spd=spd: E.matmul(PS[bd][:, 2 * h:2 * h + 2], lhsT=spd[:, h * 128:(h + 1) * 128], rhs=rcol[:, 0, :], start=True, stop=True),
                      reads=[sptok], writes=[PT[bd]], cost=mmc(1))
                A("act", lambda E, ti=ti, bd=bd: E.activation(out=decf[:, ti, :], in_=PS[bd][:, 0:8], func=AF.Exp, scale=-1.0 / 16), reads=[PT[bd]], writes=[("decf", ti)], cost=0.25)
                for h in range(4):
                    A("act", lambda E, p=p, cur=cur, h=h, ti=ti: E.activation(out=Sfb[p][:, h * 256:(h + 1) * 256], in_=cur[:, h * 256:(h + 1) * 256], func=AF.Identity,
                                                                             scale=decf[:, ti, 2 * h + 1:2 * h + 2]), reads=[(ctok, h), ("decf", ti)], writes=[("Sfb%d" % p, h)], cost=0.4)
                for d in range(2):
                    b = nb()
                    for h in range(4):
                        A("pe", lambda E, b=b, h=h, d=d, gc=gc: E.matmul(PS[b][:, h * 128:(h + 1) * 128], lhsT=Kv[:, d, h, gc], rhs=Qv[:, d, h, gc], start=True, stop=True),
                          reads=[("Kv", d, h), ("Qv", d, h)], writes=[PT[b]], cost=mmc(128))
                    mk = maskF if d == 0 else maskB
                    A("dve", lambda E, b=b, d=d, mk=mk, p=p: E.tensor_tensor(out=AT[d][p][:], in0=PS[b][:].rearrange("p (h i) -> p h i", h=4), in1=mk[:], op=ALU.mult),
                      reads=[PT[b]], writes=["AT%d_%d" % (d, p)])
                bo = [nb(), nb()]
                for c in range(8):
                    h, c2 = c // 2, c % 2
                    b = bo[c // 4]
                    o = PS[b][:, (c % 4) * 128:(c % 4 + 1) * 128]
                    vcol = slice(h * 256 + c2 * 128, h * 256 + c2 * 128 + 128)
                    A("pe", lambda E, o=o, vS=vS, vcol=vcol, h=h, p=p: E.matmul(o, lhsT=vS[:, vcol], rhs=AT[0][p][:, h, :], start=True, stop=False), reads=["kvt%d" % p, "AT0_%d" % p], writes=[PT[b]], cost=mmc(128))
                    A("pe", lambda E, o=o, p=p, vcol=vcol, h=h, gc=gc: E.matmul(o, lhsT=Sfb[p][:, vcol], rhs=Qv[:, 0, h, gc], start=False, stop=False), reads=[("Sfb%d" % p, h), ("Qv", 0, h)], writes=[PT[b]], cost=mmc(128))
                    A("pe", lambda E, o=o, vS=vS, vcol=vcol, h=h, p=p: E.matmul(o, lhsT=vS[:, vcol], rhs=AT[1][p][:, h, :], start=False, stop=False), reads=["kvt%d" % p, "AT1_%d" % p], writes=[PT[b]], cost=mmc(128))
                    A("pe", lambda E, o=o, p=p, vcol=vcol, h=h, gc=gc: E.matmul(o, lhsT=SbT[p][:, vcol], rhs=Qv[:, 1, h, gc], start=False, stop=True), reads=["SbT%d" % p, ("Qv", 1, h)], writes=[PT[b]], cost=mmc(128))
                bu = [nb(), nb()]
                for h in range(4):
                    b = bu[h // 2]
                    A("pe", lambda E, b=b, h=h, p=p, vS=vS: E.matmul(PS[b][:, (h % 2) * 256:(h % 2 + 1) * 256], lhsT=Khf[p][:, h * 128:(h + 1) * 128], rhs=vS[:, h * 256:(h + 1) * 256],
                                                                   start=True, stop=True), reads=["Khf%d" % p, "kvt%d" % p], writes=[PT[b]], cost=mmc(256))
                for h in range(4):
                    b = bu[h // 2]
                    A("dve", lambda E, b=b, h=h, ti=ti, cur=cur: E.scalar_tensor_tensor(out=cur[:, h * 256:(h + 1) * 256], in0=cur[:, h * 256:(h + 1) * 256], scalar=decf[:, ti, 2 * h:2 * h + 1],
                                                                                       in1=PS[b][:, (h % 2) * 256:(h % 2 + 1) * 256], op0=ALU.mult, op1=ALU.add),
                      reads=[(ctok, h), ("decf", ti), PT[b]], writes=[(ctok, h)], cost=0.45)
                if t % 2 == 1:
                    call = [(ctok, h_) for h_ in range(4)]
                    A("sp", lambda E, slot=slot, cur=cur: E.dma_start(out=sfo[slot], in_=cur[:]), reads=call, dma=True)
                    if slot < 5:
                        ns = slot + 1
                        nxt = Sf[ns % 2]
                        ntok = "Sf%d" % (ns % 2)
                        nall = [(ntok, h_) for h_ in range(4)]
                        if ns >= 4:
                            A("pool", lambda E, nxt=nxt: E.memset(nxt[:], 0.0), writes=nall)
                        else:
                            A("dve", lambda E, nxt=nxt, cur=cur: E.tensor_scalar(out=nxt[:], in0=cur[:], scalar1=lnk_t[:, 0:1], scalar2=None, op0=ALU.mult),
                              reads=call, writes=nall, cost=1.0)
                for hb in range(2):
                    A("act", lambda E, hb=hb, b=bo[hb], osq=osq: E.activation(out=osq[:, hb * 4:hb * 4 + 4, :], in_=PS[b][:].rearrange("p (c i) -> p c i", c=4), func=AF.Square),
                      reads=[PT[bo[hb]]], writes=[(osqt, hb)], cost=0.6)
                bn = nb()
                for h in range(4):
                    for c2 in range(2):
                        A("pe", lambda E, bn=bn, h=h, c2=c2, osq=osq: E.matmul(PS[bn][:, h * 128:(h + 1) * 128], lhsT=ones_bf[:], rhs=osq[:, 2 * h + c2, :], start=(c2 == 0), stop=(c2 == 1)),
                          reads=[(osqt, h // 2)], writes=[PT[bn]], cost=mmc(128))
                A("act", lambda E, bn=bn, rst=rst: E.activation(out=rst[:], in_=PS[bn][:].rearrange("p (h i) -> p h i", h=4), func=AF.Ln, scale=1.0 / 256, bias=epsb[:, 1:2]),
                  reads=[PT[bn]], writes=[rstt], cost=0.6)
                A("act", lambda E, rst=rst: E.activation(out=rst[:], in_=rst[:], func=AF.Exp, scale=-0.5), reads=[rstt], writes=[rstt], cost=0.6)
                for hb in range(2):
                    for c2 in range(2):
                        src = PS[bo[hb]][:].rearrange("p (h c i) -> p h c i", h=2, c=2)[:, :, c2, :]
                        dst = onf[:, hb * 4:hb * 4 + 4, :].rearrange("p (h c) i -> p h c i", c=2)[:, :, c2, :]
                        A("dve", lambda E, src=src, dst=dst, c2=c2, hb=hb, rst=rst: E.scalar_tensor_tensor(out=dst, in0=src, scalar=vecT[:, V_GNG + c2:V_GNG + c2 + 1], in1=rst[:, hb * 2:hb * 2 + 2, :],
                                                                                                 op0=ALU.mult, op1=ALU.mult), reads=[PT[bo[hb]], rstt], writes=[onft], cost=0.45)
                A("dve", lambda E, gc=gc, onf=onf: E.tensor_tensor(out=onf[:], in0=onf[:], in1=GA[:, :, gc], op=ALU.mult), reads=[onft] + GAall, writes=[onft], cost=1.2)
                A("dve", lambda E, gc=gc, onf=onf: E.tensor_tensor(out=onS[:, :, gc], in0=onf[:], in1=CB[:, :, gc], op=ALU.add), reads=[onft] + CBall, writes=[("onS", ti)], cost=1.2)
            S.tag = "g%d_wout" % g
            wo = [load_w(w_out_v, hf * 512) for hf in range(2)]
            for ti in range(4):
                t = 4 * g + ti
                p = t % 2
                cv = 0 if t < 8 else 1
                gc = slice(ti * 128, (ti + 1) * 128)
                xt = xtb[p]
                xtt = "xt2_%d" % p
                A("sp", lambda E, t=t, xt=xt: E.dma_start(out=xt[:], in_=xin[t * 128:(t + 1) * 128, :]), writes=[xtt], dma=True, cost=1.5)
                for hf in range(2):
                    b = nb("p")
                    wt, wtok = wo[hf]
                    for k in range(8):
                        A("pe", lambda E, b=b, k=k, gc=gc, wt=wt: E.matmul(PS[b][:], lhsT=onS[:, k, gc], rhs=wt[:, k, :], start=(k == 0), stop=(k == 7)), reads=[("onS", ti), wtok], writes=[PT[b]], cost=mmc(512))
                    hsl = slice(hf * 512, (hf + 1) * 512)
                    ta, tat = ntmp()
                    A("dve", lambda E, b=b, ta=ta, cv=cv, hsl=hsl: E.tensor_tensor(out=ta[:], in0=PS[b][:], in1=gaBC[:, 0, cv, hsl], op=ALU.mult), reads=[PT[b]], writes=[tat])
                    A(ENG_X1, lambda E, ta=ta, hsl=hsl, xt=xt: E.tensor_tensor(out=xt[:, hsl], in0=xt[:, hsl], in1=ta[:], op=ALU.add), reads=[tat, xtt], writes=[xtt], cost=(1.3 if ENG_X1 == "pool" else 0.6))
                A("sp", lambda E, t=t, xt=xt: E.dma_start(out=x1_s[t * 128:(t + 1) * 128, :], in_=xt[:]), reads=[xtt], writes=[("x1_s", t)], dma=True, cost=1.5)
                norm_tile(xt[:], xtt, 1, cv, xn2[p], "xn2_%d" % p, (ss, rs, xs), p, banks=(nb("p"), nb("p")), all_act=True)
                A("sp", lambda E, p=p, t=t: E.dma_start(out=xn2T_s[:, :, t * 128:(t + 1) * 128], in_=xn2[p][:]), reads=[("xn2_%d" % p, k_) for k_ in range(8)], writes=[("xn2T_s", t)], dma=True)
        S.barrier()


def _ffn(nc, S, A, T, PS, PT, psm, L):
    g_ = L
    xn2T_s, x1_s, y, msk, normf = g_["xn2T_s"], g_["x1_s"], g_["y"], g_["msk"], g_["normf"]
    w_up_v, w_gate_v, w_down_v = g_["w_up_v"], g_["w_gate_v"], g_["w_down_v"]
    identb, vecT, cwT, cwA, epsb = g_["identb"], g_["vecT"], g_["cwT"], g_["cwA"], g_["epsb"]
    PA, PB = 66, 2
    WA, WB = 1024 + 2 * PA, 512 + 2 * PB
    bank = [0]

    def nb():
        b = bank[0] % 7
        bank[0] += 1
        return b

    with ExitStack() as e3:
        hT = T("hT", [128, NFC, NTOK], BF16, e3)
        with ExitStack() as e3a:
            xn = T("xn", [128, 8, NTOK], BF16, e3a)
            wu = [T("wu%d" % i, [128, 8, 256], BF16, e3a) for i in range(3)]
            wg = [T("wg%d" % i, [128, 8, 256], BF16, e3a) for i in range(3)]
            hA = [T("hA%d" % i, [128, 3, WA], BF16, e3a) for i in range(NHB)]
            hB = [T("hB%d" % i, [128, 3, WB], BF16, e3a) for i in range(NHB)]
            mkf = T("mkf", [128, NTOK], F32, e3a)
            mk = T("mk", [128, 2, NTOK], BF16, e3a)
            dg = [T("dg%d" % i, [128, 6, 128], BF16, e3a) for i in range(NHB)]
            accp = [T("accp%d" % i, [128, 2, 512], F32, e3a) for i in range(NHB)]
            sl = [T("sl%d" % i, [128, 512], F32, e3a) for i in range(3)]
            gsb = [T("gsb%d" % i, [128, 512], F32, e3a) for i in range(3)]
            print("ffn-up sbuf remaining", nc.sbuf_bytes_remaining)

            for q in range(3):
                A("sp", lambda E, q=q: E.dma_start(out=xn[:, :, q * 512:(q + 1) * 512], in_=xn2T_s[:, :, q * 512:(q + 1) * 512]),
                  reads=[("xn2T_s", t) for t in range(4 * q, 4 * q + 4)], writes=[("xn", q)], dma=True)
            for i in range(2):
                A("sp", lambda E, i=i: E.dma_start(out=mkf[:], in_=msk[i].partition_broadcast(128)), writes=["mkf"], dma=True)
                A("dve", lambda E, i=i: E.tensor_copy(out=mk[:, i, :], in_=mkf[:]), reads=["mkf"], writes=["mk"])
            for i in range(NHB):
                A("pool", lambda E, i=i: E.memset(hA[i][:], 0.0), writes=[("hA%d" % i, v, gi) for v in range(3) for gi in range(2)])
                A("pool", lambda E, i=i: E.memset(hB[i][:], 0.0), writes=[("hB%d" % i, v, 2) for v in range(3)])
            tgroups = [(0, 512), (512, 512), (1024, 512)]
            sc = 0
            tpc = [0]
            for c in range(NFC):
                pb = c % NHB
                if c % 2 == 0:
                    i = (c // 2) % 3
                    ncol = min(256, DFF - c * 128)
                    A("pool", lambda E, i=i, c=c, ncol=ncol: E.dma_start(out=wu[i][:, :, 0:ncol], in_=w_up_v[:, :, c * 128:c * 128 + ncol]), writes=["wu%d" % i], dma=True, cost=3.0)
                    A("pool", lambda E, i=i, c=c, ncol=ncol: E.dma_start(out=wg[i][:, :, 0:ncol], in_=w_gate_v[:, :, c * 128:c * 128 + ncol]), writes=["wg%d" % i], dma=True, cost=3.0)
                i = (c // 2) % 3
                j = c % 2
                wut, wgt = "wu%d" % i, "wg%d" % i
                for dw in range(6):
                    tp = 3 + dw if dw < 3 else dw - 3
                    col = cwA[:, tp * NFC + c:tp * NFC + c + 1]
                    if dw % 2 == 0:
                        A("dve", lambda E, pb=pb, dw=dw, col=col: E.tensor_scalar(out=dg[pb][:, dw, :], in0=identb[:], scalar1=col, scalar2=None, op0=ALU.mult),
                          writes=[("dg%d" % pb, dw)], cost=0.25)
                    else:
                        A("act", lambda E, pb=pb, dw=dw, col=col: E.activation(out=dg[pb][:, dw, :], in_=identb[:], func=AF.Identity, scale=col),
                          writes=[("dg%d" % pb, dw)], cost=0.3)
                for gi, (t0, n) in enumerate(tgroups):
                    b = nb()
                    for k in range(8):
                        A("pe", lambda E, b=b, k=k, t0=t0, i=i, j=j: E.matmul(PS[b][:], lhsT=wu[i][:, k, j * 128:(j + 1) * 128], rhs=xn[:, k, t0:t0 + 512], start=(k == 0), stop=(k == 7)),
                          reads=[wut, ("xn", gi)], writes=[PT[b]], cost=mmc(512))
                    if gi < 2:
                        buf, bname, o0 = hA[pb], "hA%d" % pb, PA + t0
                    else:
                        buf, bname, o0 = hB[pb], "hB%d" % pb, PB
                    A("act", lambda E, b=b, buf=buf, o0=o0: E.copy(out=buf[:, 1, o0:o0 + 512], in_=PS[b][:]), reads=[PT[b]], writes=[(bname, 1, gi)], cost=0.65)
                    A("dve", lambda E, b=b, buf=buf, o0=o0, t0=t0: E.tensor_tensor(out=buf[:, 0, o0:o0 + 512], in0=PS[b][:], in1=mk[:, 0, t0:t0 + 512], op=ALU.mult), reads=[PT[b], "mk"], writes=[(bname, 0, gi)], cost=0.65)
                    A("dve", lambda E, b=b, buf=buf, o0=o0, t0=t0: E.tensor_tensor(out=buf[:, 2, o0:o0 + 512], in0=PS[b][:], in1=mk[:, 1, t0:t0 + 512], op=ALU.mult), reads=[PT[b], "mk"], writes=[(bname, 2, gi)], cost=0.65)
                for gi, (t0, n) in enumerate(tgroups):
                    bc = nb()
                    if gi < 2:
                        buf, bname, o0 = hA[pb], "hA%d" % pb, PA + t0
                        gis = (0, 1)
                    else:
                        buf, bname, o0 = hB[pb], "hB%d" % pb, PB
                        gis = (2,)
                    petaps = [(0, -1, 0), (0, 0, 1), (0, 1, 2)] + ([(-1, -1, 3), (-1, 0, 4), (-1, 1, 5)] if gi < 2 else [])
                    for n_, (dr, dw, di) in enumerate(petaps):
                        off = o0 + 64 * dr + dw
                        A("pe", lambda E, bc=bc, buf=buf, dw=dw, off=off, pb=pb, n_=n_, di=di, nt=len(petaps): E.matmul(PS[bc][:], lhsT=dg[pb][:, di, :], rhs=buf[:, dw + 1, off:off + 512],
                                                                                                              start=(n_ == 0), stop=(n_ == nt - 1)),
                          reads=[("dg%d" % pb, di)] + [(bname, dw + 1, g2) for g2 in gis], writes=[PT[bc]], cost=mmc(512))
                    bg = nb()
                    for k in range(8):
                        A("pe", lambda E, bg=bg, k=k, t0=t0, i=i, j=j: E.matmul(PS[bg][:], lhsT=wg[i][:, k, j * 128:(j + 1) * 128], rhs=xn[:, k, t0:t0 + 512], start=(k == 0), stop=(k == 7)),
                          reads=[wgt, ("xn", gi)], writes=[PT[bg]], cost=mmc(512))
                    q_ = sc % 3
                    sc += 1
                    s_, stok = sl[q_], "sl%d" % q_
                    gs_, gtok = gsb[q_], "gsb%d" % q_
                    A("act", lambda E, bg=bg, gs_=gs_: E.copy(out=gs_[:], in_=PS[bg][:]), reads=[PT[bg]], writes=[gtok], cost=0.65)
                    if gi < 2:
                        ac = accp[pb][:, gi, :]
                        atok = ("accp%d" % pb, gi)
                        A("act", lambda E, bc=bc, ac=ac: E.copy(out=ac, in_=PS[bc][:]), reads=[PT[bc]], writes=[atok], cost=0.65)
                        for (dr, dw) in [(1, -1), (1, 0), (1, 1)]:
                            tp = (dr + 1) * 3 + (dw + 1)
                            off = o0 + 64 * dr + dw
                            col = cwA[:, tp * NFC + c:tp * NFC + c + 1]
                            src = buf[:, dw + 1, off:off + 512]
                            rd = [(bname, dw + 1, 0), (bname, dw + 1, 1)]
                            A("dve", lambda E, ac=ac, src=src, col=col: E.scalar_tensor_tensor(out=ac, in0=src, scalar=col, in1=ac, op0=ALU.mult, op1=ALU.add),
                              reads=rd + [atok], writes=[atok], cost=0.65)
                        A("act", lambda E, ac=ac, s_=s_, c=c: E.activation(out=s_[:], in_=ac, func=AF.Silu, bias=vecT[:, V_FCB + c:V_FCB + c + 1]), reads=[atok], writes=[stok], cost=0.65)
                    else:
                        A("act", lambda E, bc=bc, s_=s_, c=c: E.activation(out=s_[:], in_=PS[bc][:], func=AF.Silu, bias=vecT[:, V_FCB + c:V_FCB + c + 1]), reads=[PT[bc]], writes=[stok], cost=0.65)
                    A("dve", lambda E, gs_=gs_, s_=s_, c=c, t0=t0: E.tensor_tensor(out=hT[:, c, t0:t0 + 512], in0=gs_[:], in1=s_[:], op=ALU.mult), reads=[gtok, stok], writes=[("hT", c, gi)], cost=0.65)
            S.barrier()
        with ExitStack() as e3b:
            wdr = T("wdr", [128, NFC, 1024], BF16, e3b)
            yb = T("yb", [128, 4, 1024], F32, e3b)
            x1t = [T("x1r%d" % i, [128, 1024], F32, e3b) for i in range(3)]
            ga2 = T("ga2", [128, 2, 1024], F32, e3b)
            nfb = T("nfb", [128, 1024], F32, e3b)
            ss = T("ss3", [128, 4], F32, e3b)
            rs = T("rs3", [128, 4], F32, e3b)
            print("ffn-down sbuf remaining", nc.sbuf_bytes_remaining)
            A("sp", lambda E: E.dma_start(out=nfb[:], in_=normf.partition_broadcast(128)), writes=["nfb"], dma=True)
            A("sp", lambda E: E.dma_start(out=ga2[:], in_=g_["ga2_s"]), writes=["ga2"], dma=True)
            for cb in range(NFC // 2):
                A("pool", lambda E, cb=cb: E.dma_start(out=wdr[:, cb * 2:cb * 2 + 2, :], in_=w_down_v[:, cb * 2:cb * 2 + 2, :]), writes=[("wdr", cb)], dma=True, cost=3.0)

            def acc_of(bi):
                if bi < 7:
                    return PS[bi][:], PT[bi]
                return psm[:], "psm"

            xc = 0
            for ps_ in range(3):
                if ps_ == 0:
                    order = [(c, ti, hf) for c in range(NFC) for ti in range(4) for hf in range(2)]
                else:
                    order = [(c, ti, hf) for ti in range(4) for hf in range(2) for c in range(NFC)]
                for (c, ti, hf) in order:
                    t = ps_ * 4 + ti
                    o, otok = acc_of(ti * 2 + hf)
                    A("pe", lambda E, o=o, c=c, t=t, hf=hf: E.matmul(o, lhsT=hT[:, c, t * 128:(t + 1) * 128], rhs=wdr[:, c, hf * 512:(hf + 1) * 512],
                                                                   start=(c == 0), stop=(c == NFC - 1)), reads=[("hT", c, t // 4), ("wdr", c // 2)], writes=[otok])
                for ti in range(4):
                    t = ps_ * 4 + ti
                    p = xc % 3
                    xc += 1
                    cv = 0 if t < 8 else 1
                    A("sp", lambda E, p=p, t=t: E.dma_start(out=x1t[p][:], in_=x1_s[t * 128:(t + 1) * 128, :]), writes=["x1r%d" % p], dma=True, cost=1.5)
                    for hf in range(2):
                        o, otok = acc_of(ti * 2 + hf)
                        hsl = slice(hf * 512, (hf + 1) * 512)
                        A("dve", lambda E, o=o, ti=ti, hsl=hsl, cv=cv: E.tensor_tensor(out=yb[:, ti, hsl], in0=o, in1=ga2[:, cv, hsl], op=ALU.mult), reads=[otok, "ga2"], writes=[("yb", ti, hf)])
                        A("pool", lambda E, ti=ti, p=p, hsl=hsl: E.tensor_tensor(out=x1t[p][:, hsl], in0=x1t[p][:, hsl], in1=yb[:, ti, hsl], op=ALU.add), reads=[("yb", ti, hf), "x1r%d" % p], writes=["x1r%d" % p])
                    A("act", lambda E, p=p, ti=ti: E.activation(out=yb[:, ti, :], in_=x1t[p][:], func=AF.Square, accum_out=ss[:, p:p + 1]), reads=["x1r%d" % p, ("yb", ti, 0), ("yb", ti, 1)],
                      writes=[("yb", ti, 0), ("yb", ti, 1), "ss3_%d" % p], cost=1.1)
                    A("act", lambda E, p=p: E.activation(out=rs[:, p:p + 1], in_=ss[:, p:p + 1], func=AF.Ln, scale=1.0 / 1024, bias=epsb[:, 0:1]), reads=["ss3_%d" % p, "epsb0"], writes=["rs3_%d" % p], cost=0.25)
                    A("act", lambda E, p=p: E.activation(out=rs[:, p:p + 1], in_=rs[:, p:p + 1], func=AF.Exp, scale=-0.5), reads=["rs3_%d" % p], writes=["rs3_%d" % p], cost=0.25)
                    A("dve", lambda E, p=p: E.scalar_tensor_tensor(out=x1t[p][:], in0=x1t[p][:], scalar=rs[:, p:p + 1], in1=nfb[:], op0=ALU.mult, op1=ALU.mult),
                      reads=["x1r%d" % p, "rs3_%d" % p, "nfb"], writes=["x1r%d" % p], cost=1.2)
                    A("sp", lambda E, p=p, t=t: E.dma_start(out=y[t * 128:(t + 1) * 128, :], in_=x1t[p][:]), reads=["x1r%d" % p], dma=True, cost=1.5)


_CACHE = {}


def _layout(inputs):
    f = lambda a: np.ascontiguousarray(np.asarray(a, dtype=np.float32))
    xp, xs_, c, cctx = f(inputs["x_prompt"]), f(inputs["x_sample"]), f(inputs["c"]), f(inputs["c_ctx"])
    sf, sb = f(inputs["state_gla_fwd"]), f(inputs["state_gla_bwd"])
    vecs = np.concatenate([
        f(inputs["b_ada"])[0].reshape(48, 128), f(inputs["norm1_g"])[0].reshape(8, 128), f(inputs["norm2_g"])[0].reshape(8, 128),
        f(inputs["conv_mix_w"])[0].reshape(24, 128), f(inputs["ffn_conv_b"])[0].reshape(22, 128), f(inputs["gla_norm_g"])[0].reshape(2, 128)], axis=0)
    cwr = f(inputs["ffn_conv_w"])[0].reshape(9 * 22, 128)
    wgk = np.stack([np.concatenate([f(inputs["w_gk_f"])[0], f(inputs["b_gk_f"])], axis=0),
                    np.concatenate([f(inputs["w_gk_b"])[0], f(inputs["b_gk_b"])], axis=0)], axis=1)
    shared = dict(vecs=f(vecs), cwr=cwr, w_ada=f(inputs["w_ada"])[0], b_ada=f(inputs["b_ada"])[0], w_in=f(inputs["w_in"])[0], wgk=f(wgk),
                  w_out=f(inputs["w_out"])[0], w_up=f(inputs["ffn_w_up"])[0], w_gate=f(inputs["ffn_w_gate"])[0], w_down=f(inputs["ffn_w_down"])[0],
                  normf=f(inputs["normf_g"]))
    maps, plan = [], []
    tok = np.arange(NTOK)
    for core in range(8):
        if core < 4:
            pids = [2 * core, 2 * core + 1]
            x = np.concatenate([xs_[core]] + [xp[i] for i in pids], axis=0)
            cv2 = np.stack([c[core], cctx])
            s0f = sf[core, 0].transpose(1, 0, 2).reshape(128, 1024)
            s0b = sb[core, 0].transpose(1, 0, 2).reshape(128, 1024)
            link = 1.0
            w_of = np.where(tok < 1024, tok % 64, tok % 256)
            wmax = np.where(tok < 1024, 63, 255)
            flag = np.ones(9, np.float32)
            plan.append(("s", core, [(4, pids[0]), (5, pids[1])]))
        else:
            pids = [8 + 6 * (core - 4) + i for i in range(6)]
            x = np.concatenate([xp[i] for i in pids], axis=0)
            cv2 = np.stack([cctx, cctx])
            s0f = np.zeros((128, 1024), np.float32)
            s0b = np.zeros((128, 1024), np.float32)
            link = 0.0
            w_of = tok % 256
            wmax = np.full(NTOK, 255)
            flag = np.array([0, 0, 0, 1, 1, 1, 0, 0, 0], np.float32)
            plan.append(("p", core, [(i, pids[i]) for i in range(6)]))
        lnk = np.tile(np.array([[link, -(1.0 - link), -1.0, 1.0]], np.float32), (128, 1))
        msk = np.stack([(w_of != wmax), (w_of != 0)]).astype(np.float32)
        m = dict(xin=f(x), cv2=f(cv2), s0f=f(s0f), s0b=f(s0b), lnk=lnk, msk=msk, flagA=np.tile(flag[None, :], (128, 1)))
        m.update(shared)
        maps.append(m)
    return maps, plan


def kernel(**inputs):
    if "nc" not in _CACHE:
        _CACHE["nc"] = build_program()[0]
    nc = _CACHE["nc"]
    maps, plan = _layout(inputs)
    res = run_bass_kernel_spmd(nc, maps, core_ids=list(range(8)))
    y_prompt = np.zeros((32, 256, 1024), np.float32)
    y_sample = np.zeros((4, 1024, 1024), np.float32)
    nsf = np.zeros((32, 1, 4, 128, 256), np.float32)
    nsb = np.zeros((32, 1, 4, 128, 256), np.float32)
    for (kind, core, slots) in plan:
        r = res.results[core]
        yy = np.asarray(r["y"])
        if kind == "s":
            y_sample[core] = yy[:1024]
        for (slot, pid) in slots:
            y_prompt[pid] = yy[slot * 256:(slot + 1) * 256]
            nsf[pid, 0] = np.asarray(r["sfo"])[slot].reshape(128, 4, 256).transpose(1, 0, 2)
            nsb[pid, 0] = np.asarray(r["sbo"])[slot].reshape(128, 4, 256).transpose(1, 0, 2)
    return (y_prompt, y_sample, nsf, nsb)
```

```python
import numpy as np
from contextlib import ExitStack
import concourse.bass as bass
import concourse.mybir as mybir
from concourse.bass_utils import run_bass_kernel_spmd

F32 = mybir.dt.float32
BF16 = mybir.dt.bfloat16
AF = mybir.ActivationFunctionType
ALU = mybir.AluOpType
ND = 6

NT = 12
NTOK = 1536
DFF = 2816
NFC = 22
EPS = 1e-6
C_Q, C_K, C_V, C_G, C_CF, C_CB, C_B, C_C, C_X, C_GA, C_GB = 0, 512, 1024, 2048, 3072, 3088, 3104, 4128, 5152, 6176, 7200
V_BADA, V_N1, V_N2, V_CMW, V_FCB, V_GNG = 0, 48, 56, 64, 88, 110


class Op:
    __slots__ = ("eng", "fn", "deps", "odeps", "needs_inc", "dma", "dsem", "dval", "ms", "seq", "phase", "cost", "lat", "fin", "tag")

    def __init__(self, eng, fn, dma):
        self.eng = eng
        self.fn = fn
        self.dma = dma
        self.deps = []
        self.odeps = []
        self.needs_inc = False
        self.dsem = None
        self.dval = 0
        self.ms = 0
        self.fin = 0.0


PRIO_W = 0.035
NB1 = 3
NHB = 2
ENG_GA = "dve"
ENG_CB = "dve"
ENG_X1 = "dve"
BANKS_P = [0, 1, 2, 3]
BANKS_G = [4, 5, 6]
DEF_COST = {"pe": 0.235, "act": 0.62, "dve": 0.62, "pool": 1.4}


class Sched:
    def __init__(self, nc, es, dummies=None):
        self.nc = nc
        self.engs = {"pe": nc.tensor, "act": nc.scalar, "dve": nc.vector,
                     "pool": nc.gpsimd, "sp": nc.sync}
        self.sem = {e: es.enter_context(nc.semaphore("s_" + e)) for e in self.engs}
        self.dsems = {q: [es.enter_context(nc.semaphore("d_%s%d" % (q, i))) for i in range(ND)]
                      for q in ("sp", "pool")}
        self.last_w = {}
        self.readers = {}
        self.phases = [[]]
        self.nseq = 0
        self.dummies = dummies
        self.reorder = True
        self.prio_w = PRIO_W

    def add(self, eng, fn, reads=(), writes=(), dma=False, cost=None):
        op = Op(eng, fn, dma)
        deps = {}
        for t in reads:
            w = self.last_w.get(t)
            if w is not None:
                deps[id(w)] = w
            if isinstance(t, str) and t.startswith("ps"):
                for r in self.readers.get(t, ()):
                    if r.eng != eng:
                        deps[id(r)] = r
        for t in writes:
            w = self.last_w.get(t)
            if w is not None:
                deps[id(w)] = w
            for r in self.readers.get(t, ()):
                deps[id(r)] = r
        ph = len(self.phases) - 1
        op.phase = ph
        for d in deps.values():
            if d.phase != ph:
                continue
            op.odeps.append(d)
            if d.eng == "pe" and eng == "pe" and not d.dma and not dma:
                continue
            op.deps.append(d)
        for t in reads:
            self.readers.setdefault(t, []).append(op)
        for t in writes:
            self.last_w[t] = op
            self.readers[t] = []
        op.seq = self.nseq
        op.tag = getattr(self, "tag", "")
        self.nseq += 1
        if dma:
            op.cost = cost if cost is not None else 1.0
            op.lat = op.cost + 2.0
        else:
            op.cost = cost if cost is not None else DEF_COST[eng]
            op.lat = op.cost
        self.phases[-1].append(op)
        return op

    addb = add

    def barrier(self, dummies=None):
        self.phases.append([])

    def _schedule(self, ops):
        import heapq
        order = {e: [] for e in self.engs}
        if not self.reorder:
            for op in ops:
                order[op.eng].append(op)
            return order
        users = {}
        indeg = {}
        for op in ops:
            indeg[id(op)] = len(op.odeps)
            for d in op.odeps:
                users.setdefault(id(d), []).append(op)
        free = {e: 0.0 for e in self.engs}

        free["dmabw"] = 0.0

        def est(op):
            t = free[op.eng]
            if op.dma and free["dmabw"] > t:
                t = free["dmabw"]
            for d in op.odeps:
                x = d.fin + (0.45 if d.eng != op.eng else 0.1)
                if x > t:
                    t = x
            return t

        tail = {}
        for op in reversed(ops):
            m = 0.0
            for u in users.get(id(op), ()):
                x = tail[id(u)]
                if x > m:
                    m = x
            tail[id(op)] = m + op.lat
        PW = self.prio_w

        def key_of(op, t):
            return t - PW * tail[id(op)]

        heap = []
        for op in ops:
            if indeg[id(op)] == 0:
                heapq.heappush(heap, (key_of(op, est(op)), op.seq, op))
        n = 0
        while heap:
            key, _, op = heapq.heappop(heap)
            t = est(op)
            k2 = key_of(op, t)
            if k2 > key + 1e-9:
                heapq.heappush(heap, (k2, op.seq, op))
                continue
            if op.dma:
                free[op.eng] = t + (1.0 if op.eng == "pool" else 0.15)
                free["dmabw"] = t + op.cost
            else:
                free[op.eng] = t + op.cost
            op.fin = t + op.lat
            order[op.eng].append(op)
            n += 1
            for u in users.get(id(op), ()):
                indeg[id(u)] -= 1
                if indeg[id(u)] == 0:
                    heapq.heappush(heap, (key_of(u, est(u)), u.seq, u))
        assert n == len(ops), (n, len(ops))
        if getattr(self, "report", False):
            busy = {e: sum(o.cost for o in order[e] if not o.dma) for e in order}
            busy["dma"] = sum(o.cost for e in order for o in order[e] if o.dma)
            print("phase busy", {e: round(v) for e, v in busy.items()}, "span", round(max(free.values())))
            tags = {}
            for o in ops:
                a = tags.setdefault(o.tag, [1e9, 0.0, 0])
                a[0] = min(a[0], o.fin - o.lat)
                a[1] = max(a[1], o.fin)
                a[2] += 1
            for k, v in sorted(tags.items(), key=lambda kv: kv[1][0]):
                print("   tag %-12s start %7.1f end %7.1f n=%d" % (k, v[0], v[1], v[2]))
        self.est_time = getattr(self, "est_time", []) + [max(o.fin for o in ops) if ops else 0.0]
        return order

    def emit(self, block):
        dm = self.dummies
        final = {e: [] for e in self.engs}
        pend = {"pe": [], "sp": []}
        nph = len(self.phases)
        for pi, ops in enumerate(self.phases):
            order = self._schedule(ops)
            for e in self.engs:
                lst = order[e]
                if e in pend and pend[e] and lst:
                    lst[0].deps = list(lst[0].deps) + pend[e]
                    pend[e] = []
                final[e].extend(lst)
            if pi < nph - 1:
                firsts = []
                for e in ("act", "dve", "pool"):
                    t = dm[e]
                    o = Op(e, (lambda E, t=t, e=e: (E.memzero(t) if e == "act" else E.memset(t, 0.0))), False)
                    o.phase = -1
                    if e == "pool":
                        o.deps = [x for x in ops if x.dma]
                    firsts.append(o)
                    final[e].append(o)
                seconds = []
                lastpe = order["pe"][-1] if order["pe"] else None
                for e in ("act", "dve", "pool"):
                    t = dm[e]
                    o = Op(e, (lambda E, t=t, e=e: (E.memzero(t) if e == "act" else E.memset(t, 0.0))), False)
                    o.phase = -1
                    o.deps = list(firsts) + ([lastpe] if lastpe is not None else [])
                    seconds.append(o)
                    final[e].append(o)
                pend = {"pe": list(seconds), "sp": list(seconds)}
        dcnt = {q: [0] * ND for q in self.dsems}
        for q in self.dsems:
            rr = 0
            last = [None] * ND
            for op in final[q]:
                if not op.dma:
                    continue
                s = rr % ND
                rr += 1
                if last[s] is not None:
                    op.deps = list(op.deps) + [last[s]]
                dcnt[q][s] += 1
                op.dsem = self.dsems[q][s]
                op.dval = 16 * dcnt[q][s]
                last[s] = op
        for e in self.engs:
            for op in final[e]:
                for d in op.deps:
                    if not d.dma:
                        d.needs_inc = True
        for e, lst in final.items():
            c = 0
            for op in lst:
                if op.needs_inc and not op.dma:
                    c += 1
                    op.ms = c
        stats = {}
        names = {"pe": "tensor", "act": "scalar", "dve": "vector", "pool": "gpsimd", "sp": "sync"}
        for e in self.engs:
            lst = final[e]
            E_sem = self.sem[e]

            def run(E, lst=lst, e=e, E_sem=E_sem):
                seen = {}
                nw = 0
                for op in lst:
                    for d in op.deps:
                        if d.dma:
                            key = ("d", id(d.dsem))
                            sem, val = d.dsem, d.dval
                        else:
                            key = ("c", d.eng)
                            sem, val = self.sem[d.eng], d.ms
                        if seen.get(key, 0) >= val:
                            continue
                        seen[key] = val
                        E.wait_ge(sem, val)
                        nw += 1
                    ins = op.fn(E)
                    if op.dma:
                        ins.then_inc(op.dsem, 16)
                    elif op.needs_inc:
                        ins.then_inc(E_sem, 1)
                if e == "sp":
                    for q in self.dsems:
                        for s in range(ND):
                            if dcnt[q][s]:
                                E.wait_ge(self.dsems[q][s], 16 * dcnt[q][s])
                stats[e] = (len(lst), nw)

            getattr(block, names[e])(run)
        stats["est_us"] = [round(x) for x in getattr(self, "est_time", [])]
        return stats


def build_program(debug=False, stop_after=99, reorder=True):
    nc = bass.Bass("TRN2", target_bir_lowering=False)

    def din(name, shape, dt=F32):
        return nc.dram_tensor(name, shape, dt, kind="ExternalInput").ap()

    def dout(name, shape, dt=F32):
        return nc.dram_tensor(name, shape, dt, kind="ExternalOutput").ap()

    def dscr(name, shape, dt):
        if debug:
            return nc.dram_tensor(name, shape, dt, kind="ExternalOutput").ap()
        return nc.dram_tensor(name, shape, dt).ap()

    xin = din("xin", [NTOK, 1024])
    cv2 = din("cv2", [2, 1024])
    s0f = din("s0f", [128, 1024])
    s0b = din("s0b", [128, 1024])
    lnk = din("lnk", [128, 4])
    msk = din("msk", [2, NTOK])
    flagA = din("flagA", [128, 9])
    vecs = din("vecs", [112, 128])
    cwr = din("cwr", [198, 128])
    w_ada = din("w_ada", [1024, 6144])
    b_ada = din("b_ada", [6144])
    w_in = din("w_in", [1024, 8224])
    wgk = din("wgk", [17, 2, 512])
    w_out = din("w_out", [1024, 1024])
    w_up = din("w_up", [1024, DFF])
    w_gate = din("w_gate", [1024, DFF])
    w_down = din("w_down", [DFF, 1024])
    normf = din("normf", [1024])
    y = dout("y", [NTOK, 1024])
    sfo = dout("sfo", [6, 128, 1024])
    sbo = dout("sbo", [6, 128, 1024])
    xnT_s = dscr("xnT_s", [128, 8, NTOK], BF16)
    xn2T_s = dscr("xn2T_s", [128, 8, NTOK], BF16)
    SbS_s = dscr("SbS_s", [NT, 128, 1024], BF16)
    x1_s = dscr("x1_s", [NTOK, 1024], F32)
    ga2_s = dscr("ga2_s", [128, 2, 1024], F32)
    kv_s = dscr("kv_s", [NT, 128, 1536], BF16)

    w_in_v = w_in.rearrange("(k p) n -> p k n", p=128)
    w_ada_v = w_ada.rearrange("(k p) n -> p k n", p=128)
    w_out_v = w_out.rearrange("(k p) n -> p k n", p=128)
    w_up_v = w_up.rearrange("(k p) n -> p k n", p=128)
    w_gate_v = w_gate.rearrange("(k p) n -> p k n", p=128)
    w_down_v = w_down.rearrange("(c p) n -> p c n", p=128)

    with ExitStack() as es:
        S = Sched(nc, es)

        def T(name, shape, dt, st=es):
            return st.enter_context(nc.sbuf_tensor(name, shape, dt))

        PS = [es.enter_context(nc.psum_tensor("ps%d" % i, [128, 512], F32)) for i in range(7)]
        psm = es.enter_context(nc.psum_tensor("psm", [128, 512], F32))
        PT = ["ps%d" % i for i in range(7)]

        ident = T("ident", [128, 128], F32)
        identb = T("identb", [128, 128], BF16)
        epsb = T("epsb", [128, 2], F32)
        vecT = T("vecT", [128, 112], F32)
        cwT = T("cwT", [128, 198], F32)
        cwA = T("cwA", [128, 198], F32)
        lnk_t = T("lnk_t", [128, 4], F32)
        flg_t = T("flg_t", [128, 9], F32)
        eff = T("eff", [128, 4, 8, 2], F32)
        gaBC = T("gaBC", [128, 1, 2, 1024], F32)
        dmy = T("dmy", [128, 8], F32)
        dummies = {"act": dmy[:, 0:1], "dve": dmy[:, 1:2], "pool": dmy[:, 2:3]}
        S.dummies = dummies
        S.reorder = reorder
        e12 = es.enter_context(ExitStack())
        maskF = T("maskF", [128, 4, 128], F32, e12)
        maskB = T("maskB", [128, 4, 128], F32, e12)
        Rf = T("Rf", [128, 2, 128], BF16, e12)
        Rb = T("Rb", [128, 2, 128], BF16, e12)
        T3f = T("T3f", [128, 128], BF16, e12)
        T3b = T("T3b", [128, 128], BF16, e12)
        ones_bf = T("ones_bf", [128, 128], BF16, e12)
        rcol = T("rcol", [128, 2, 2], BF16, e12)
        cfT = T("cfT", [17, NTOK], BF16, e12)
        cbT = T("cbT", [17, NTOK], BF16, e12)
        wgk_t = T("wgk_t", [17, 2, 512], BF16, e12)

        block = es.enter_context(nc.Block())
        A = S.addb

        def pool_mask(out_ap, tok, pattern, cm, base):
            A("pool", lambda E: E.memset(out_ap, 1.0), writes=[tok])
            A("pool", lambda E: E.affine_select(out=out_ap, in_=out_ap, pattern=pattern, compare_op=ALU.is_ge,
                                                fill=0.0, base=base, channel_multiplier=cm), reads=[tok], writes=[tok])

        e0 = ExitStack()
        if True:
            refF = T("refF", [128, 128], F32, e0)
            refB = T("refB", [128, 128], F32, e0)
            vrow = T("vrow", [112, 128], F32, e0)
            cwr0 = T("cwr0", [128, 128], F32, e0)
            cwr1 = T("cwr1", [70, 128], F32, e0)
            cT = T("cT", [128, 8, 2], F32, e0)
            cTb = T("cTb", [128, 8, 2], BF16, e0)
            cBC = T("cBC", [128, 8, 2, 128], BF16, e0)
            wab = [T("wab%d" % i, [128, 8, 1024], BF16, e0) for i in range(2)]
            bbc = T("bbc", [128, 2, 1024], F32, e0)
            modT = T("modT", [128, 4, 8, 2], F32, e0)
            ga2t = T("ga2t", [128, 1, 2, 1024], F32, e0)

            A("pool", lambda E: E.memset(ident[:], 1.0), writes=["ident"])
            A("pool", lambda E: E.affine_select(out=ident[:], in_=ident[:], pattern=[[-1, 128]], compare_op=ALU.is_equal,
                                                fill=0.0, base=0, channel_multiplier=1), reads=["ident"], writes=["ident"])
            A("dve", lambda E: E.tensor_copy(out=identb[:], in_=ident[:]), reads=["ident"], writes=["identb"])
            pool_mask(maskF[:], "maskF", [[0, 4], [1, 128]], -1, 0)
            pool_mask(maskB[:], "maskB", [[0, 4], [-1, 128]], 1, 0)
            pool_mask(refF[:], "refF", [[0, 128]], -1, 64)
            pool_mask(refB[:], "refB", [[0, 128]], 1, -63)
            A("pool", lambda E: E.memset(ones_bf[:], 1.0), writes=["ones_bf"])
            A("pool", lambda E: E.memset(rcol[:], 1.0), writes=["rcol"])
            A("dve", lambda E: E.tensor_copy(out=rcol[:, 0, 1:2], in_=refF[:, 0:1]), reads=["refF", "rcol"], writes=["rcol"])
            A("dve", lambda E: E.tensor_copy(out=rcol[:, 1, 1:2], in_=refB[:, 0:1]), reads=["refB", "rcol"], writes=["rcol"])
            A("pool", lambda E: E.memset(epsb[:, 0:1], EPS), writes=["epsb0"])
            A("pool", lambda E: E.memset(epsb[:, 1:2], EPS * 128.0), writes=["epsb1"])
            A("pool", lambda E: E.memset(cfT[:], 1.0), writes=["cfT"])
            A("pool", lambda E: E.memset(cbT[:], 1.0), writes=["cbT"])
            A("dve", lambda E: E.tensor_tensor(out=Rf[:, 0, :], in0=maskF[:, 0, :], in1=refF[:], op=ALU.subtract), reads=["maskF", "refF"], writes=["Rf"])
            A("dve", lambda E: E.tensor_copy(out=Rf[:, 1, :], in_=maskF[:, 0, :]), reads=["maskF"], writes=["Rf"])
            A("dve", lambda E: E.tensor_tensor(out=Rb[:, 0, :], in0=maskB[:, 0, :], in1=refB[:], op=ALU.subtract), reads=["maskB", "refB"], writes=["Rb"])
            A("dve", lambda E: E.tensor_copy(out=Rb[:, 1, :], in_=maskB[:, 0, :]), reads=["maskB"], writes=["Rb"])
            A("dve", lambda E: E.tensor_scalar(out=T3f[:], in0=maskF[:, 0, :], scalar1=-1.0, scalar2=1.0, op0=ALU.mult, op1=ALU.add), reads=["maskF"], writes=["T3f"])
            A("dve", lambda E: E.tensor_scalar(out=T3b[:], in0=maskB[:, 0, :], scalar1=-1.0, scalar2=1.0, op0=ALU.mult, op1=ALU.add), reads=["maskB"], writes=["T3b"])
            A("sp", lambda E: E.dma_start(out=vrow[:], in_=vecs), writes=["vrow"], dma=True)
            A("sp", lambda E: E.dma_start(out=cwr0[:], in_=cwr[0:128, :]), writes=["cwr0"], dma=True)
            A("sp", lambda E: E.dma_start(out=cwr1[:], in_=cwr[128:198, :]), writes=["cwr1"], dma=True)
            A("sp", lambda E: E.dma_start(out=lnk_t[:], in_=lnk), writes=["lnk"], dma=True)
            A("sp", lambda E: E.dma_start(out=flg_t[:], in_=flagA), writes=["flg"], dma=True)
            for c_ in range(2):
                A("sp", lambda E, c_=c_: E.dma_start(out=cT[:, :, c_], in_=cv2[c_].rearrange("(k p) -> p k", p=128), allow_slow_non_contiguous=True), writes=["cT"], dma=True)
            A("pool", lambda E: E.dma_start(out=wgk_t[:], in_=wgk), writes=["wgk"], dma=True)
            A("pe", lambda E: E.transpose(out=PS[0][:, 0:112], in_=vrow[:], identity=ident[0:112, 0:112]), reads=["vrow", "ident"], writes=[PT[0]])
            A("dve", lambda E: E.tensor_copy(out=vecT[:], in_=PS[0][:, 0:112]), reads=[PT[0]], writes=["vecT"])
            A("pe", lambda E: E.transpose(out=PS[1][:, 0:128], in_=cwr0[:], identity=ident[:]), reads=["cwr0", "ident"], writes=[PT[1]])
            A("pe", lambda E: E.transpose(out=PS[1][:, 128:198], in_=cwr1[:], identity=ident[0:70, 0:70]), reads=["cwr1", "ident"], writes=[PT[1]])
            A("dve", lambda E: E.tensor_copy(out=cwT[:], in_=PS[1][:, 0:198]), reads=[PT[1]], writes=["cwT"])
            A("dve", lambda E: E.tensor_tensor(out=cwA[:].rearrange("p (t c) -> p t c", c=NFC), in0=cwT[:].rearrange("p (t c) -> p t c", c=NFC),
                                               in1=flg_t[:].unsqueeze(2).to_broadcast([128, 9, NFC]), op=ALU.mult), reads=["cwT", "flg"], writes=["cwA"])
            A("act", lambda E: E.activation(out=cT[:], in_=cT[:], func=AF.Silu), reads=["cT"], writes=["cT"])
            A("dve", lambda E: E.tensor_copy(out=cTb[:], in_=cT[:]), reads=["cT"], writes=["cTb"])
            A("dve", lambda E: E.tensor_copy(out=cBC[:], in_=cTb[:].unsqueeze(3).to_broadcast([128, 8, 2, 128])), reads=["cTb"], writes=["cBC"])
            order = [(1, "pp", 0), (0, "pp", 1), (4, "pp", 2), (3, "pp", 3), (2, "bc", 0), (5, "bc", 1)]

            def mod_block(n, extra_reads=()):
                v, kind, slot = order[n]
                wb = wab[n % 2]
                wt = "wab%d" % (n % 2)
                A("pool", lambda E, wb=wb, v=v: E.dma_start(out=wb[:], in_=w_ada_v[:, :, v * 1024:(v + 1) * 1024]), reads=list(extra_reads), writes=[wt], dma=True, cost=12.0)
                if kind == "pp":
                    pst = PS[2 + (n % 2)]
                    ptk = PT[2 + (n % 2)]
                    for j in range(8):
                        for k in range(8):
                            A("pe", lambda E, wb=wb, j=j, k=k, pst=pst: E.matmul(pst[:, j * 2:j * 2 + 2], lhsT=wb[:, k, j * 128:(j + 1) * 128], rhs=cTb[:, k, :],
                                                                               start=(k == 0), stop=(k == 7)), reads=[wt, "cTb"], writes=[ptk], cost=mmc(2))
                    A("dve", lambda E, pst=pst, slot=slot, v=v: E.tensor_tensor(out=modT[:, slot, :, :], in0=pst[:, 0:16].rearrange("p (j c) -> p j c", c=2),
                                                                               in1=vecT[:, V_BADA + v * 8:V_BADA + v * 8 + 8].unsqueeze(2).to_broadcast([128, 8, 2]), op=ALU.add),
                      reads=[ptk, "vecT"], writes=["modT%d" % slot], cost=0.2)
                else:
                    A("sp", lambda E, slot=slot, v=v: E.dma_start(out=bbc[:, slot, :], in_=b_ada[v * 1024:(v + 1) * 1024].partition_broadcast(128)), reads=list(extra_reads), writes=["bbc%d" % slot], dma=True)
                    for cv in range(2):
                        for hf in range(2):
                            pst = PS[4 + hf]
                            ptk = PT[4 + hf]
                            for k in range(8):
                                A("pe", lambda E, wb=wb, k=k, cv=cv, hf=hf, pst=pst: E.matmul(pst[:], lhsT=cBC[:, k, cv, :], rhs=wb[:, k, hf * 512:(hf + 1) * 512],
                                                                                           start=(k == 0), stop=(k == 7)), reads=[wt, "cBC"], writes=[ptk])
                            gdst = gaBC if slot == 0 else ga2t
                            A("dve", lambda E, slot=slot, cv=cv, hf=hf, pst=pst, gdst=gdst: E.tensor_tensor(out=gdst[:, 0, cv, hf * 512:(hf + 1) * 512], in0=pst[:],
                                                                                              in1=bbc[:, slot, hf * 512:(hf + 1) * 512], op=ALU.add),
                              reads=[ptk, "bbc%d" % slot], writes=["gaBC%d" % slot])
                    if slot == 1:
                        A("sp", lambda E: E.dma_start(out=ga2_s, in_=ga2t[:, 0, :, :]), reads=["gaBC1"], writes=["ga2_s"], dma=True)

            def mod_eff(i):
                sc_slot, sh_slot, goff = [(0, 1, V_N1), (2, 3, V_N2)][i]
                A("dve", lambda E: E.scalar_tensor_tensor(out=eff[:, 2 * i, :, :], in0=modT[:, sc_slot, :, :], scalar=1.0,
                                                          in1=vecT[:, goff:goff + 8].unsqueeze(2).to_broadcast([128, 8, 2]),
                                                          op0=ALU.add, op1=ALU.mult), reads=["modT%d" % sc_slot, "vecT"], writes=[("eff", i)], cost=0.2)
                A("dve", lambda E: E.tensor_copy(out=eff[:, 2 * i + 1, :, :], in_=modT[:, sh_slot, :, :]), reads=["modT%d" % sh_slot], writes=[("eff", i)], cost=0.2)

            mod_block(0)
            mod_block(1)
            mod_eff(0)

        def norm_tile(xt_ap, xt_tok, which, cv, dst, dst_tok, scr, p, banks=(0, 1), all_act=False):
            ss, rs, xs = scr
            xs_tok = "xs_" + xs.name if hasattr(xs, "name") else "xs"
            A("act", lambda E: E.activation(out=xs[:], in_=xt_ap, func=AF.Square, accum_out=ss[:, p:p + 1]), reads=[xt_tok], writes=[xs_tok, "ss%d" % p], cost=0.85)
            A("act", lambda E: E.activation(out=rs[:, p:p + 1], in_=ss[:, p:p + 1], func=AF.Ln, scale=1.0 / 1024, bias=epsb[:, 0:1]), reads=["ss%d" % p, "epsb0"], writes=["rs%d" % p], cost=0.2)
            A("act", lambda E: E.activation(out=rs[:, p:p + 1], in_=rs[:, p:p + 1], func=AF.Exp, scale=-0.5), reads=["rs%d" % p], writes=["rs%d" % p], cost=0.2)
            A("dve", lambda E: E.tensor_scalar(out=xs[:], in0=xt_ap, scalar1=rs[:, p:p + 1], scalar2=None, op0=ALU.mult), reads=[xt_tok, "rs%d" % p], writes=[xs_tok], cost=1.1)
            for k in range(8):
                b = banks[k // 4]
                A("pe", lambda E, k=k, b=b: E.transpose(out=PS[b][:, (k % 4) * 128:(k % 4 + 1) * 128], in_=xs[:, k * 128:(k + 1) * 128], identity=ident[:]),
                  reads=[xs_tok, "ident"], writes=[PT[b]], cost=0.113)
            for k in range(8):
                b = banks[k // 4]
                src = PS[b][:, (k % 4) * 128:(k % 4 + 1) * 128]
                sc = eff[:, 2 * which, k, cv:cv + 1]
                sh = eff[:, 2 * which + 1, k, cv:cv + 1]
                if k < 4 and not all_act:
                    A("dve", lambda E, k=k, src=src, sc=sc, sh=sh: E.tensor_scalar(out=dst[:, k, :], in0=src, scalar1=sc, scalar2=sh, op0=ALU.mult, op1=ALU.add),
                      reads=[PT[b], ("eff", which)], writes=[(dst_tok, k)], cost=0.3)
                else:
                    A("act", lambda E, k=k, src=src, sc=sc, sh=sh: E.activation(out=dst[:, k, :], in_=src, func=AF.Identity, scale=sc, bias=sh),
                      reads=[PT[b], ("eff", which)], writes=[(dst_tok, k)], cost=0.35)

        with ExitStack() as e1:
            wk = T("wk", [128, 8, 512], BF16, e1)
            wv = T("wv", [128, 8, 1024], BF16, e1)
            wcd = T("wcd", [128, 8, 32], BF16, e1)
            xt = [T("p1xt%d" % i, [128, 1024], F32, e1) for i in range(NB1)]
            xsb = [T("p1xs%d" % i, [128, 1024], F32, e1) for i in range(NB1)]
            ss = T("ss", [128, 4], F32, e1)
            rs = T("rs", [128, 4], F32, e1)
            xnt = [T("xnt%d" % i, [128, 8, 128], BF16, e1) for i in range(NB1)]
            kvb = [T("kvb%d" % i, [128, 1536], BF16, e1) for i in range(NB1)]
            kSb = [kvb[i][:, 0:512] for i in range(NB1)]
            vSb = [kvb[i][:, 512:1536] for i in range(NB1)]
            ex1b = [T("ex1_%d" % i, [128, 512], F32, e1) for i in range(NB1)]
            spbb = [T("spb%d" % i, [128, 512], BF16, e1) for i in range(NB1)]
            e3b_ = [T("e3_%d" % i, [128, 512], F32, e1) for i in range(NB1)]
            Khb = [T("Kh%d" % i, [128, 512], BF16, e1) for i in range(NB1)]
            decb = [T("dec%d" % i, [128, 8], F32, e1) for i in range(NB1)]
            ones_c = T("ones_c", [128, 1], BF16, e1)
            Sb = [T("Sb%d" % i, [128, 1024], F32, e1) for i in range(2)]
            sbs = [T("sbs%d" % i, [128, 1024], BF16, e1) for i in range(NB1)]

            A("pool", lambda E: E.memset(ones_c[:], 1.0), writes=["ones_c"])
            A("pool", lambda E: E.dma_start(out=wk[:], in_=w_in_v[:, :, C_K:C_K + 512]), writes=["wk"], dma=True, cost=6.0)
            A("pool", lambda E: E.dma_start(out=wcd[:], in_=w_in_v[:, :, C_CF:C_CF + 32]), writes=["wcd"], dma=True)
            A("pool", lambda E: E.dma_start(out=wv[:], in_=w_in_v[:, :, C_V:C_V + 1024]), writes=["wv"], dma=True, cost=12.0)
            A("pool", lambda E: E.memset(Sb[1][:], 0.0), writes=[("Sb1", h_) for h_ in range(4)])
            bank1 = [0]

            def nb1():
                b_ = bank1[0] % 7
                bank1[0] += 1
                return b_

            cnt = 0
            for t in reversed(range(NT)):
                p = cnt % NB1
                cnt += 1
                slot = t // 2
                cv = 0 if t < 8 else 1
                tc_ = slice(t * 128, (t + 1) * 128)
                xs, kS, vS, ex1, spb, e3, Kh, dec = xsb[p], kSb[p], vSb[p], ex1b[p], spbb[p], e3b_[p], Khb[p], decb[p]
                sfx = "_%d" % p
                A("sp", lambda E, p=p, t=t: E.dma_start(out=xt[p][:], in_=xin[t * 128:(t + 1) * 128, :]), writes=["xt%d" % p], dma=True, cost=1.5)
                norm_tile(xt[p][:], "xt%d" % p, 0, cv, xnt[p], "xnt%d" % p, (ss, rs, xs), p, banks=(nb1(), nb1()))
                A("sp", lambda E, p=p, tc_=tc_: E.dma_start(out=xnT_s[:, :, tc_], in_=xnt[p][:]), reads=[("xnt%d" % p, k_) for k_ in range(8)], writes=[("xnT_s", t)], dma=True)
                if cnt in (3, 5, 7, 9):
                    nblk = 2 + (cnt - 3) // 2
                    mod_block(nblk, extra_reads=[("xnt%d" % p, 7)])
                    if nblk == 3:
                        mod_eff(1)
                X = xnt[p]
                xtok = "xnt%d" % p
                bk, bv0, bv1 = nb1(), nb1(), nb1()
                for k in range(8):
                    A("pe", lambda E, k=k, X=X, bk=bk: E.matmul(PS[bk][:], lhsT=X[:, k, :], rhs=wk[:, k, :], start=(k == 0), stop=(k == 7)), reads=[(xtok, k), "wk"], writes=[PT[bk]])
                for hf in range(2):
                    bv = (bv0, bv1)[hf]
                    for k in range(8):
                        A("pe", lambda E, k=k, X=X, hf=hf, bv=bv: E.matmul(PS[bv][:], lhsT=X[:, k, :], rhs=wv[:, k, hf * 512:(hf + 1) * 512], start=(k == 0), stop=(k == 7)),
                          reads=[(xtok, k), "wv"], writes=[PT[bv]])
                for c in range(2):
                    for k in range(8):
                        A("pe", lambda E, k=k, X=X, c=c, p=p: E.matmul(psm[0:16, (p % 2) * 256 + c * 128:(p % 2) * 256 + (c + 1) * 128], lhsT=wcd[:, k, c * 16:(c + 1) * 16], rhs=X[:, k, :], start=(k == 0), stop=(k == 7)),
                          reads=[(xtok, k), "wcd"], writes=["psm"], cost=0.1)
                A("act", lambda E, kS=kS, bk=bk: E.copy(out=kS, in_=PS[bk][:]), reads=[PT[bk]], writes=["kS" + sfx])
                A("act", lambda E, vS=vS, bv0=bv0: E.copy(out=vS[:, 0:512], in_=PS[bv0][:]), reads=[PT[bv0]], writes=["vS0" + sfx])
                A("dve", lambda E, vS=vS, bv1=bv1: E.tensor_copy(out=vS[:, 512:1024], in_=PS[bv1][:]), reads=[PT[bv1]], writes=["vS1" + sfx])
                A("dve", lambda E, tc_=tc_, p=p: E.tensor_copy(out=cfT[0:16, tc_], in_=psm[0:16, (p % 2) * 256:(p % 2) * 256 + 128]), reads=["psm", "cfT"], writes=[("cfT", t)], cost=0.2)
                A("dve", lambda E, tc_=tc_, p=p: E.tensor_copy(out=cbT[0:16, tc_], in_=psm[0:16, (p % 2) * 256 + 128:(p % 2) * 256 + 256]), reads=["psm", "cbT"], writes=[("cbT", t)], cost=0.2)
                bl = nb1()
                A("pe", lambda E, tc_=tc_, bl=bl: E.matmul(PS[bl][:], lhsT=cbT[0:17, tc_], rhs=wgk_t[:, 1, :], start=True, stop=True), reads=[("cbT", t), "cbT", "wgk"], writes=[PT[bl]])
                A("act", lambda E, ex1=ex1, bl=bl: E.activation(out=ex1[:], in_=PS[bl][:], func=AF.Exp, scale=-1.0), reads=[PT[bl]], writes=["ex1" + sfx])
                A("act", lambda E, ex1=ex1, spb=spb: E.activation(out=spb[:], in_=ex1[:], func=AF.Ln, bias=1.0), reads=["ex1" + sfx], writes=["spb" + sfx])
                be3, bd = nb1(), nb1()
                A("pe", lambda E, spb=spb, be3=be3: E.matmul(PS[be3][:], lhsT=T3b[:], rhs=spb[:], start=True, stop=True), reads=["T3b", "spb" + sfx], writes=[PT[be3]])
                for h in range(4):
                    A("pe", lambda E, h=h, spb=spb, bd=bd: E.matmul(PS[bd][:, 2 * h:2 * h + 2], lhsT=spb[:, h * 128:(h + 1) * 128], rhs=rcol[:, 1, :], start=True, stop=True),
                      reads=["spb" + sfx, "rcol"], writes=[PT[bd]], cost=0.1)
                A("act", lambda E, e3=e3, be3=be3: E.activation(out=e3[:], in_=PS[be3][:], func=AF.Exp, scale=-1.0 / 16), reads=[PT[be3]], writes=["e3" + sfx])
                A("act", lambda E, dec=dec, bd=bd: E.activation(out=dec[:], in_=PS[bd][:, 0:8], func=AF.Exp, scale=-1.0 / 16), reads=[PT[bd]], writes=["dec" + sfx], cost=0.3)
                A("dve", lambda E, Kh=Kh, kS=kS, e3=e3: E.tensor_tensor(out=Kh[:], in0=kS, in1=e3[:], op=ALU.mult), reads=["kS" + sfx, "e3" + sfx], writes=["Kh" + sfx])
                A("sp", lambda E, p=p, t=t: E.dma_start(out=kv_s[t], in_=kvb[p][:]), reads=["kS" + sfx, "vS0" + sfx, "vS1" + sfx], writes=[("kv_s", t)], dma=True)
                bu = (nb1(), nb1())
                for h in range(4):
                    b_ = bu[h // 2]
                    A("pe", lambda E, h=h, b_=b_, Kh=Kh, vS=vS: E.matmul(PS[b_][:, (h % 2) * 256:(h % 2 + 1) * 256], lhsT=Kh[:, h * 128:(h + 1) * 128], rhs=vS[:, h * 256:(h + 1) * 256],
                                                        start=True, stop=True), reads=["Kh" + sfx, "vS0" + sfx, "vS1" + sfx], writes=[PT[b_]], cost=0.15)
                cur = Sb[slot % 2]
                ctok = "Sb%d" % (slot % 2)
                for h in range(4):
                    A("act", lambda E, p=p, cur=cur, h=h, dec=dec: E.activation(out=sbs[p][:, h * 256:(h + 1) * 256], in_=cur[:, h * 256:(h + 1) * 256], func=AF.Identity,
                                                                               scale=dec[:, 2 * h + 1:2 * h + 2]), reads=[(ctok, h), "dec" + sfx], writes=[("sbs%d" % p, h)], cost=0.4)
                A("sp", lambda E, p=p, t=t: E.dma_start(out=SbS_s[t], in_=sbs[p][:]), reads=[("sbs%d" % p, h_) for h_ in range(4)], writes=[("SbS_s", t)], dma=True)
                for h in range(4):
                    b_ = bu[h // 2]
                    A("dve", lambda E, h=h, b_=b_, cur=cur, dec=dec: E.scalar_tensor_tensor(out=cur[:, h * 256:(h + 1) * 256], in0=cur[:, h * 256:(h + 1) * 256], scalar=dec[:, 2 * h:2 * h + 1],
                                                                                in1=PS[b_][:, (h % 2) * 256:(h % 2 + 1) * 256], op0=ALU.mult, op1=ALU.add),
                      reads=[(ctok, h), "dec" + sfx, PT[b_]], writes=[(ctok, h)], cost=0.45)
                if t % 2 == 0:
                    call = [(ctok, h_) for h_ in range(4)]
                    A("sp", lambda E, slot=slot, cur=cur: E.dma_start(out=sbo[slot], in_=cur[:]), reads=call, dma=True)
                    if slot > 0:
                        ns = slot - 1
                        nxt = Sb[ns % 2]
                        ntok = "Sb%d" % (ns % 2)
                        nall = [(ntok, h_) for h_ in range(4)]
                        if ns == 4:
                            A("pool", lambda E, nxt=nxt: E.memset(nxt[:], 0.0), writes=nall)
                        elif ns == 3:
                            A("sp", lambda E, nxt=nxt: E.dma_start(out=nxt[:], in_=s0b), writes=nall, dma=True)
                        else:
                            A("dve", lambda E, nxt=nxt, cur=cur: E.tensor_scalar(out=nxt[:], in0=cur[:], scalar1=lnk_t[:, 0:1], scalar2=None, op0=ALU.mult),
                              reads=call + ["lnk"], writes=nall)
            S.barrier(dummies)
        e0.close()

        if stop_after >= 2:
            _pass2(nc, S, A, T, PS, PT, psm, locals())
        e12.close()
        if stop_after >= 3:
            _ffn(nc, S, A, T, PS, PT, psm, locals())
        stats = S.emit(block)
    return nc, stats


def mmc(n):
    if n >= 512:
        return 0.235
    if n >= 256:
        return 0.19
    if n >= 128:
        return 0.113
    return 0.03


def _pass2(nc, S, A, T, PS, PT, psm, L):
    g_ = L
    xin, xnT_s, xn2T_s, SbS_s, x1_s, s0f, sfo, kv_s = g_["xin"], g_["xnT_s"], g_["xn2T_s"], g_["SbS_s"], g_["x1_s"], g_["s0f"], g_["sfo"], g_["kv_s"]
    w_in_v, w_out_v = g_["w_in_v"], g_["w_out_v"]
    ident, maskF, maskB, Rf, Rb, T3f = g_["ident"], g_["maskF"], g_["maskB"], g_["Rf"], g_["Rb"], g_["T3f"]
    rcol = g_["rcol"]
    ones_bf, epsb, vecT, lnk_t, eff, gaBC, cfT, cbT, wgk_t = g_["ones_bf"], g_["epsb"], g_["vecT"], g_["lnk_t"], g_["eff"], g_["gaBC"], g_["cfT"], g_["cbT"], g_["wgk_t"]
    norm_tile = g_["norm_tile"]
    with ExitStack() as e2:
        xg = T("xg", [128, 8, 512], BF16, e2)
        wb = [T("wb%d" % i, [128, 8, 512], BF16, e2) for i in range(3)]
        EE = [T("EE%d" % i, [128, 4, 512], BF16, e2) for i in range(2)]
        E2 = [T("E2%d" % i, [128, 4, 512], BF16, e2) for i in range(2)]
        decf = T("decf", [128, 4, 8], F32, e2)
        Qv = T("Qv", [128, 2, 4, 512], BF16, e2)
        Kv = T("Kv", [128, 2, 4, 512], BF16, e2)
        kvt = [T("kvt%d" % i, [128, 1536], BF16, e2) for i in range(2)]
        e3t = [T("e3t%d" % i, [128, 512], BF16, e2) for i in range(2)]
        Khf = [T("Khf%d" % i, [128, 512], BF16, e2) for i in range(2)]
        sp = [[T("sp0_%d" % i, [128, 512], BF16, e2) for i in range(4)], [T("sp1_%d" % i, [128, 512], BF16, e2) for i in range(2)]]
        AT = [[T("AT%d_%d" % (d, i), [128, 4, 128], BF16, e2) for i in range(2)] for d in range(2)]
        GA = T("GA", [128, 8, 512], BF16, e2)
        CB = T("CB", [128, 8, 512], BF16, e2)
        onS = T("onS", [128, 8, 512], BF16, e2)
        onfb = [T("onf%d" % i, [128, 8, 128], F32, e2) for i in range(2)]
        osqb = [T("osq%d" % i, [128, 8, 128], BF16, e2) for i in range(2)]
        rstb = [T("rst%d" % i, [128, 4, 128], F32, e2) for i in range(2)]
        tmpA = [T("tmpA%d" % i, [128, 512], F32, e2) for i in range(2)]
        cc = [T("cc%d" % i, [128, 512], F32, e2) for i in range(2)]
        z = T("z", [128, 4, 514], F32, e2)
        acc = T("acc", [128, 4, 512], F32, e2)
        zl = T("zl", [128, 8], F32, e2)
        ztmp = T("ztmp", [128, 8], F32, e2)
        xh = T("xh", [128, 8, 1], BF16, e2)
        ones_c = T("ones_c2", [128, 1], BF16, e2)
        Sf = [T("Sf%d" % i, [128, 1024], F32, e2) for i in range(2)]
        Sfb = [T("Sfb%d" % i, [128, 1024], BF16, e2) for i in range(2)]
        SbT = [T("SbT%d" % i, [128, 1024], BF16, e2) for i in range(2)]
        xtb = [T("xt2_%d" % i, [128, 1024], F32, e2) for i in range(2)]
        xs = T("xs2", [128, 1024], F32, e2)
        ss = T("ss2", [128, 2], F32, e2)
        rs = T("rs2", [128, 2], F32, e2)
        xn2 = [T("xn2_%d" % i, [128, 8, 128], BF16, e2) for i in range(2)]
        print("pass2 sbuf remaining", nc.sbuf_bytes_remaining)

        A("pool", lambda E: E.memset(ones_c[:], 1.0), writes=["ones_c2"])
        A("pool", lambda E: E.memset(z[:], 0.0), writes=[("z", j) for j in range(4)] + ["zh"])
        A("sp", lambda E: E.dma_start(out=Sf[0][:], in_=s0f), writes=[("Sf0", h_) for h_ in range(4)], dma=True)
        wcnt = [0]

        def load_w(src_v, c0):
            i = wcnt[0] % 3
            wcnt[0] += 1
            A("pool", lambda E, i=i: E.dma_start(out=wb[i][:], in_=src_v[:, :, c0:c0 + 512]), writes=["wb%d" % i], dma=True, cost=6.0)
            return wb[i], "wb%d" % i

        bank = {"p": 0, "g": 0}
        pools = {"p": BANKS_P, "g": BANKS_G}

        def nb(kind="g"):
            b = bank["g"] % 7
            bank["g"] += 1
            return b

        def proj_fm(wt, wtok, j, rhs, rtok):
            b = nb("p")
            for k in range(8):
                A("pe", lambda E, k=k, b=b: E.matmul(PS[b][:], lhsT=wt[:, k, j * 128:(j + 1) * 128], rhs=rhs[:, k, :], start=(k == 0), stop=(k == 7)),
                  reads=[wtok, rtok], writes=[PT[b]], cost=mmc(512))
            return b

        tac = [0]

        def ntmp():
            i = tac[0] % 2
            tac[0] += 1
            return tmpA[i], "tmpA%d" % i

        tcount = 0
        for g in range(3):
            S.tag = "g%d_gate" % g
            A("sp", lambda E, g=g: E.dma_start(out=xg[:], in_=xnT_s[:, :, g * 512:(g + 1) * 512]), writes=["xg"], dma=True, cost=3.0)
            for ti in range(4):
                t = 4 * g + ti
                p = t % 2
                tc_ = slice(t * 128, (t + 1) * 128)
                gc = slice(ti * 128, (ti + 1) * 128)
                for d in range(2):
                    cT_ = cfT if d == 0 else cbT
                    R = Rf if d == 0 else Rb
                    si = ti if d == 0 else p
                    spd = sp[d][si]
                    sptok = "sp%d_%d" % (d, si)
                    b = nb()
                    A("pe", lambda E, b=b, cT_=cT_, d=d, tc_=tc_: E.matmul(PS[b][:], lhsT=cT_[0:17, tc_], rhs=wgk_t[:, d, :], start=True, stop=True), writes=[PT[b]], cost=mmc(512))
                    A("act", lambda E, b=b: E.activation(out=PS[b][:], in_=PS[b][:], func=AF.Exp, scale=-1.0), reads=[PT[b]], writes=[PT[b]], cost=0.6)
                    A("act", lambda E, b=b, spd=spd: E.activation(out=spd[:], in_=PS[b][:], func=AF.Ln, bias=1.0), reads=[PT[b]], writes=[sptok], cost=0.6)
                    b = nb()
                    for h in range(4):
                        A("pe", lambda E, b=b, h=h, spd=spd, R=R: E.matmul(PS[b][:, h * 128:(h + 1) * 128], lhsT=spd[:, h * 128:(h + 1) * 128], rhs=R[:, 0, :], start=True, stop=True),
                          reads=[sptok], writes=[PT[b]], cost=mmc(128))
                    A("act", lambda E, b=b, d=d, gc=gc: E.activation(out=EE[d][:, :, gc], in_=PS[b][:].rearrange("p (h i) -> p h i", h=4), func=AF.Exp, scale=-1.0 / 16),
                      reads=[PT[b]], writes=[("EE", d, ti)], cost=0.6)
                    A("act", lambda E, b=b, d=d, gc=gc: E.activation(out=E2[d][:, :, gc], in_=PS[b][:].rearrange("p (h i) -> p h i", h=4), func=AF.Exp, scale=1.0 / 16),
                      reads=[PT[b]], writes=[("E2", d, ti)], cost=0.6)
            S.tag = "g%d_qk" % g
            wq, wqt = load_w(w_in_v, C_Q)
            for h in range(4):
                b = proj_fm(wq, wqt, h, xg, "xg")
                for d in range(2):
                    A("dve", lambda E, b=b, d=d, h=h: E.tensor_tensor(out=Qv[:, d, h, :], in0=PS[b][:], in1=EE[d][:, h, :], op=ALU.mult),
                      reads=[PT[b]] + [("EE", d, i) for i in range(4)], writes=[("Qv", d, h)])
            wkk, wkt = load_w(w_in_v, C_K)
            for h in range(4):
                b = proj_fm(wkk, wkt, h, xg, "xg")
                for d in range(2):
                    A("dve", lambda E, b=b, d=d, h=h: E.tensor_tensor(out=Kv[:, d, h, :], in0=PS[b][:], in1=E2[d][:, h, :], op=ALU.mult),
                      reads=[PT[b]] + [("E2", d, i) for i in range(4)], writes=[("Kv", d, h)])
            S.tag = "g%d_ogate" % g
            for hf in range(2):
                wG, wGt = load_w(w_in_v, C_G + hf * 512)
                wA, wAt = load_w(w_in_v, C_GA + hf * 512)
                tas = []
                for j in range(4):
                    b = proj_fm(wG, wGt, j, xg, "xg")
                    ta, tat = ntmp()
                    A("act", lambda E, b=b, ta=ta: E.activation(out=ta[:], in_=PS[b][:], func=AF.Silu), reads=[PT[b]], writes=[tat], cost=0.6)
                    b = proj_fm(wA, wAt, j, xg, "xg")
                    A("act", lambda E, b=b, j=j, hf=hf: E.activation(out=GA[:, hf * 4 + j, :], in_=PS[b][:], func=AF.Sigmoid), reads=[PT[b]], writes=[("GA", hf * 4 + j)], cost=0.6)
                    A(ENG_GA, lambda E, ta=ta, j=j, hf=hf: E.tensor_tensor(out=GA[:, hf * 4 + j, :], in0=GA[:, hf * 4 + j, :], in1=ta[:], op=ALU.mult),
                      reads=[tat, ("GA", hf * 4 + j)], writes=[("GA", hf * 4 + j)], cost=(1.3 if ENG_GA == "pool" else 0.6))
            S.tag = "g%d_conv" % g
            for hf in range(2):
                wcc, wcct = load_w(w_in_v, C_C + hf * 512)
                wcx, wcxt = load_w(w_in_v, C_X + hf * 512)
                for j in range(4):
                    b = proj_fm(wcc, wcct, j, xg, "xg")
                    A("act", lambda E, b=b, j=j: E.copy(out=cc[j % 2][:], in_=PS[b][:]), reads=[PT[b]], writes=["cc%d" % (j % 2)], cost=0.6)
                    b = proj_fm(wcx, wcxt, j, xg, "xg")
                    A("dve", lambda E, b=b, j=j: E.tensor_tensor(out=z[:, j, 1:513], in0=PS[b][:], in1=cc[j % 2][:], op=ALU.mult), reads=[PT[b], "cc%d" % (j % 2)], writes=[("z", j)])
                hs = slice(hf * 4, hf * 4 + 4)
                zall = [("z", j) for j in range(4)]
                if g == 0:
                    A("sp", lambda E: E.dma_start(out=xh[:], in_=xnT_s[:, :, 512:513], allow_slow_non_contiguous=True), writes=["xh"], dma=True)
                    for wi, (wt, wtok) in enumerate([(wcc, wcct), (wcx, wcxt)]):
                        for j in range(4):
                            for k in range(8):
                                A("pe", lambda E, wt=wt, wi=wi, j=j, k=k: E.matmul(psm[:, 300 + wi * 4 + j:301 + wi * 4 + j], lhsT=wt[:, k, j * 128:(j + 1) * 128], rhs=xh[:, k, :],
                                                                                 start=(k == 0), stop=(k == 7)), reads=[wtok, "xh"], writes=["psm"], cost=mmc(1))
                    A("dve", lambda E: E.tensor_copy(out=ztmp[:, 0:4], in_=psm[:, 300:304]), reads=["psm"], writes=["ztmp"], cost=0.2)
                    A("dve", lambda E: E.tensor_tensor(out=ztmp[:, 0:4], in0=ztmp[:, 0:4], in1=psm[:, 304:308], op=ALU.mult), reads=["psm", "ztmp"], writes=["ztmp"], cost=0.2)
                    A("dve", lambda E: E.tensor_scalar(out=z[:, :, 513], in0=ztmp[:, 0:4], scalar1=lnk_t[:, 0:1], scalar2=None, op0=ALU.mult), reads=["ztmp"] + zall, writes=["zh"] + zall, cost=0.2)
                    A("dve", lambda E, hs=hs: E.tensor_copy(out=zl[:, hs], in_=z[:, :, 512]), reads=zall, writes=["zl"], cost=0.2)
                elif g == 1:
                    A("dve", lambda E, hs=hs: E.tensor_scalar(out=z[:, :, 0], in0=zl[:, hs], scalar1=lnk_t[:, 0:1], scalar2=None, op0=ALU.mult), reads=["zl"] + zall, writes=["zh"] + zall, cost=0.2)
                    A("pool", lambda E: E.memset(z[:, :, 513], 0.0), reads=zall, writes=["zh"] + zall, cost=0.2)
                else:
                    A("pool", lambda E: E.memset(z[:, :, 0], 0.0), reads=zall, writes=["zh"] + zall, cost=0.2)
                w0 = vecT[:, V_CMW + 0 + hf * 4:V_CMW + 0 + hf * 4 + 4]
                w2 = vecT[:, V_CMW + 16 + hf * 4:V_CMW + 16 + hf * 4 + 4]
                for j in range(4):
                    c = hf * 4 + j
                    A("act", lambda E, j=j, c=c: E.activation(out=acc[:, j, :], in_=z[:, j, 1:513], func=AF.Identity, scale=vecT[:, V_CMW + 8 + c:V_CMW + 9 + c]),
                      reads=[("z", j)], writes=[("acc", j)], cost=0.7)
                    A("dve", lambda E, j=j, c=c: E.scalar_tensor_tensor(out=acc[:, j, :], in0=z[:, j, 0:512], scalar=vecT[:, V_CMW + c:V_CMW + c + 1], in1=acc[:, j, :],
                                                                       op0=ALU.mult, op1=ALU.add), reads=[("z", j), ("acc", j)], writes=[("acc", j)])
                    A("dve", lambda E, j=j, c=c: E.scalar_tensor_tensor(out=acc[:, j, :], in0=z[:, j, 2:514], scalar=vecT[:, V_CMW + 16 + c:V_CMW + 17 + c], in1=acc[:, j, :],
                                                                       op0=ALU.mult, op1=ALU.add), reads=[("z", j), ("acc", j)], writes=[("acc", j)])
                ncol = 1 if g < 2 else 2
                aall = [("acc", j) for j in range(4)]
                A("dve", lambda E, w0=w0: E.tensor_tensor(out=ztmp[:, 0:4], in0=z[:, :, 256], in1=w0, op=ALU.mult), reads=zall + ["ztmp"], writes=["ztmp"], cost=0.2)
                A("dve", lambda E, ncol=ncol: E.scalar_tensor_tensor(out=acc[:, :, 256], in0=ztmp[:, 0:4], scalar=lnk_t[:, ncol:ncol + 1], in1=acc[:, :, 256], op0=ALU.mult, op1=ALU.add),
                  reads=["ztmp"] + aall, writes=aall, cost=0.2)
                A("dve", lambda E, w2=w2: E.tensor_tensor(out=ztmp[:, 4:8], in0=z[:, :, 257], in1=w2, op=ALU.mult), reads=zall + ["ztmp"], writes=["ztmp"], cost=0.2)
                A("dve", lambda E, ncol=ncol: E.scalar_tensor_tensor(out=acc[:, :, 255], in0=ztmp[:, 4:8], scalar=lnk_t[:, ncol:ncol + 1], in1=acc[:, :, 255], op0=ALU.mult, op1=ALU.add),
                  reads=["ztmp"] + aall, writes=aall, cost=0.2)
                wcb, wcbt = load_w(w_in_v, C_B + hf * 512)
                for j in range(4):
                    b = proj_fm(wcb, wcbt, j, xg, "xg")
                    A("dve", lambda E, b=b, j=j: E.tensor_tensor(out=acc[:, j, :], in0=PS[b][:], in1=acc[:, j, :], op=ALU.mult), reads=[PT[b], ("acc", j)], writes=[("acc", j)])
                wgb, wgbt = load_w(w_in_v, C_GB + hf * 512)
                for j in range(4):
                    c = hf * 4 + j
                    b = proj_fm(wgb, wgbt, j, xg, "xg")
                    ta, tat = ntmp()
                    A("act", lambda E, b=b, ta=ta: E.activation(out=ta[:], in_=PS[b][:], func=AF.Sigmoid), reads=[PT[b]], writes=[tat], cost=0.6)
                    A(ENG_CB, lambda E, j=j, c=c, ta=ta: E.tensor_tensor(out=CB[:, c, :], in0=acc[:, j, :], in1=ta[:], op=ALU.mult), reads=[("acc", j), tat], writes=[("CB", c)], cost=(1.3 if ENG_CB == "pool" else 0.6))
            S.tag = "g%d_gla" % g
            GAall = [("GA", c) for c in range(8)]
            CBall = [("CB", c) for c in range(8)]
            for ti in range(4):
                t = 4 * g + ti
                slot = t // 2
                p = t % 2
                tcount += 1
                gc = slice(ti * 128, (ti + 1) * 128)
                cur = Sf[slot % 2]
                ctok = "Sf%d" % (slot % 2)
                vS = kvt[p][:, 512:1536]
                onf, osq, rst = onfb[p], osqb[p], rstb[p]
                onft, osqt, rstt = "onf%d" % p, "osq%d" % p, "rst%d" % p
                Qall = [("Qv", vi, h) for vi in range(4) for h in range(4)]
                Kall = [("Kv", d, h) for d in range(2) for h in range(4)]
                A("sp", lambda E, p=p, t=t: E.dma_start(out=SbT[p][:], in_=SbS_s[t]), writes=["SbT%d" % p], dma=True, cost=1.0)
                A("sp", lambda E, p=p, t=t: E.dma_start(out=kvt[p][:], in_=kv_s[t]), writes=["kvt%d" % p], dma=True, cost=1.2)
                spd = sp[0][ti]
                sptok = "sp0_%d" % ti
                b = nb()
                A("pe", lambda E, b=b, spd=spd: E.matmul(PS[b][:], lhsT=T3f[:], rhs=spd[:], start=True, stop=True), reads=[sptok], writes=[PT[b]], cost=mmc(512))
                A("act", lambda E, b=b, p=p: E.activation(out=e3t[p][:], in_=PS[b][:], func=AF.Exp, scale=-1.0 / 16), reads=[PT[b]], writes=["e3t%d" % p], cost=0.6)
                A("dve", lambda E, p=p: E.tensor_tensor(out=Khf[p][:], in0=kvt[p][:, 0:512], in1=e3t[p][:], op=ALU.mult), reads=["kvt%d" % p, "e3t%d" % p], writes=["Khf%d" % p], cost=0.45)
                bd = nb()
                for h in range(4):
                    A("pe", lambda E, h=h, bd=bd, spd=spd: E.matmul(PS[bd][:, 2 * h:2 * h + 2], lhsT=spd[:, h * 128:(h + 1) * 128], rhs=rcol[:, 0, :], start=True, stop=True),
                      reads=[sptok], writes=[PT[bd]], cost=mmc(1))
                A("act", lambda E, ti=ti, bd=bd: E.activation(out=decf[:, ti, :], in_=PS[bd][:, 0:8], func=AF.Exp, scale=-1.0 / 16), reads=[PT[bd]], writes=[("decf", ti)], cost=0.25)
                for h in range(4):
                    A("act", lambda E, p=p, cur=cur, h=h, ti=ti: E.activation(out=Sfb[p][:, h * 256:(h + 1) * 256], in_=cur[:, h * 256:(h + 1) * 256], func=AF.Identity,
                                                                             scale=decf[:, ti, 2 * h + 1:2 * h + 2]), reads=[(ctok, h), ("decf", ti)], writes=[("Sfb%d" % p, h)], cost=0.4)
                for d in range(2):
                    b = nb()
                    for h in range(4):
                        A("pe", lambda E, b=b, h=h, d=d, gc=gc: E.matmul(PS[b][:, h * 128:(h + 1) * 128], lhsT=Kv[:, d, h, gc], rhs=Qv[:, d, h, gc], start=True, stop=True),
                          reads=[("Kv", d, h), ("Qv", d, h)], writes=[PT[b]], cost=mmc(128))
                    mk = maskF if d == 0 else maskB
                    A("dve", lambda E, b=b, d=d, mk=mk, p=p: E.tensor_tensor(out=AT[d][p][:], in0=PS[b][:].rearrange("p (h i) -> p h i", h=4), in1=mk[:], op=ALU.mult),
                      reads=[PT[b]], writes=["AT%d_%d" % (d, p)])
                bo = [nb(), nb()]
                for c in range(8):
                    h, c2 = c // 2, c % 2
                    b = bo[c // 4]
                    o = PS[b][:, (c % 4) * 128:(c % 4 + 1) * 128]
                    vcol = slice(h * 256 + c2 * 128, h * 256 + c2 * 128 + 128)
                    A("pe", lambda E, o=o, vS=vS, vcol=vcol, h=h, p=p: E.matmul(o, lhsT=vS[:, vcol], rhs=AT[0][p][:, h, :], start=True, stop=False), reads=["kvt%d" % p, "AT0_%d" % p], writes=[PT[b]], cost=mmc(128))
                    A("pe", lambda E, o=o, p=p, vcol=vcol, h=h, gc=gc: E.matmul(o, lhsT=Sfb[p][:, vcol], rhs=Qv[:, 0, h, gc], start=False, stop=False), reads=[("Sfb%d" % p, h), ("Qv", 0, h)], writes=[PT[b]], cost=mmc(128))
                    A("pe", lambda E, o=o, vS=vS, vcol=vcol, h=h, p=p: E.matmul(o, lhsT=vS[:, vcol], rhs=AT[1][p][:, h, :], start=False, stop=False), reads=["kvt%d" % p, "AT1_%d" % p], writes=[PT[b]], cost=mmc(128))
                    A("pe", lambda E, o=o, p=p, vcol=vcol, h=h, gc=gc: E.matmul(o, lhsT=SbT[p][:, vcol], rhs=Qv[:, 1, h, gc], start=False, stop=True), reads=["SbT%d" % p, ("Qv", 1, h)], writes=[PT[b]], cost=mmc(128))
                bu = [nb(), nb()]
                for h in range(4):
                    b = bu[h // 2]
                    A("pe", lambda E, b=b, h=h, p=p, vS=vS: E.matmul(PS[b][:, (h % 2) * 256:(h % 2 + 1) * 256], lhsT=Khf[p][:, h * 128:(h + 1) * 128], rhs=vS[:, h * 256:(h + 1) * 256],
                                                                   start=True, stop=True), reads=["Khf%d" % p, "kvt%d" % p], writes=[PT[b]], cost=mmc(256))
                for h in range(4):
                    b = bu[h // 2]
                    A("dve", lambda E, b=b, h=h, ti=ti, cur=cur: E.scalar_tensor_tensor(out=cur[:, h * 256:(h + 1) * 256], in0=cur[:, h * 256:(h + 1) * 256], scalar=decf[:, ti, 2 * h:2 * h + 1],
                                                                                       in1=PS[b][:, (h % 2) * 256:(h % 2 + 1) * 256], op0=ALU.mult, op1=ALU.add),
                      reads=[(ctok, h), ("decf", ti), PT[b]], writes=[(ctok, h)], cost=0.45)
                if t % 2 == 1:
                    call = [(ctok, h_) for h_ in range(4)]
                    A("sp", lambda E, slot=slot, cur=cur: E.dma_start(out=sfo[slot], in_=cur[:]), reads=call, dma=True)
                    if slot < 5:
                        ns = slot + 1
                        nxt = Sf[ns % 2]
                        ntok = "Sf%d" % (ns % 2)
                        nall = [(ntok, h_) for h_ in range(4)]
                        if ns >= 4:
                            A("pool", lambda E, nxt=nxt: E.memset(nxt[:], 0.0), writes=nall)
                        else:
                            A("dve", lambda E, nxt=nxt, cur=cur: E.tensor_scalar(out=nxt[:], in0=cur[:], scalar1=lnk_t[:, 0:1], scalar2=None, op0=ALU.mult),
                              reads=call, writes=nall, cost=1.0)
                for hb in range(2):
                    A("act", lambda E, hb=hb, b=bo[hb], osq=osq: E.activation(out=osq[:, hb * 4:hb * 4 + 4, :], in_=PS[b][:].rearrange("p (c i) -> p c i", c=4), func=AF.Square),
                      reads=[PT[bo[hb]]], writes=[(osqt, hb)], cost=0.6)
                bn = nb()
                for h in range(4):
                    for c2 in range(2):
                        A("pe", lambda E, bn=bn, h=h, c2=c2, osq=osq: E.matmul(PS[bn][:, h * 128:(h + 1) * 128], lhsT=ones_bf[:], rhs=osq[:, 2 * h + c2, :], start=(c2 == 0), stop=(c2 == 1)),
                          reads=[(osqt, h // 2)], writes=[PT[bn]], cost=mmc(128))
                A("act", lambda E, bn=bn, rst=rst: E.activation(out=rst[:], in_=PS[bn][:].rearrange("p (h i) -> p h i", h=4), func=AF.Ln, scale=1.0 / 256, bias=epsb[:, 1:2]),
                  reads=[PT[bn]], writes=[rstt], cost=0.6)
                A("act", lambda E, rst=rst: E.activation(out=rst[:], in_=rst[:], func=AF.Exp, scale=-0.5), reads=[rstt], writes=[rstt], cost=0.6)
                for hb in range(2):
                    for c2 in range(2):
                        src = PS[bo[hb]][:].rearrange("p (h c i) -> p h c i", h=2, c=2)[:, :, c2, :]
                        dst = onf[:, hb * 4:hb * 4 + 4, :].rearrange("p (h c) i -> p h c i", c=2)[:, :, c2, :]
                        A("dve", lambda E, src=src, dst=dst, c2=c2, hb=hb, rst=rst: E.scalar_tensor_tensor(out=dst, in0=src, scalar=vecT[:, V_GNG + c2:V_GNG + c2 + 1], in1=rst[:, hb * 2:hb * 2 + 2, :],
                                                                                                 op0=ALU.mult, op1=ALU.mult), reads=[PT[bo[hb]], rstt], writes=[onft], cost=0.45)
                A("dve", lambda E, gc=gc, onf=onf: E.tensor_tensor(out=onf[:], in0=onf[:], in1=GA[:, :, gc], op=ALU.mult), reads=[onft] + GAall, writes=[onft], cost=1.2)
                A("dve", lambda E, gc=gc, onf=onf: E.tensor_tensor(out=onS[:, :, gc], in0=onf[:], in1=CB[:, :, gc], op=ALU.add), reads=[onft] + CBall, writes=[("onS", ti)], cost=1.2)
            S.tag = "g%d_wout" % g
            wo = [load_w(w_out_v, hf * 512) for hf in range(2)]
            for ti in range(4):
                t = 4 * g + ti
                p = t % 2
                cv = 0 if t < 8 else 1
                gc = slice(ti * 128, (ti + 1) * 128)
                xt = xtb[p]
                xtt = "xt2_%d" % p
                A("sp", lambda E, t=t, xt=xt: E.dma_start(out=xt[:], in_=xin[t * 128:(t + 1) * 128, :]), writes=[xtt], dma=True, cost=1.5)
                for hf in range(2):
                    b = nb("p")
                    wt, wtok = wo[hf]
                    for k in range(8):
                        A("pe", lambda E, b=b, k=k, gc=gc, wt=wt: E.matmul(PS[b][:], lhsT=onS[:, k, gc], rhs=wt[:, k, :], start=(k == 0), stop=(k == 7)), reads=[("onS", ti), wtok], writes=[PT[b]], cost=mmc(512))
                    hsl = slice(hf * 512, (hf + 1) * 512)
                    ta, tat = ntmp()
                    A("dve", lambda E, b=b, ta=ta, cv=cv, hsl=hsl: E.tensor_tensor(out=ta[:], in0=PS[b][:], in1=gaBC[:, 0, cv, hsl], op=ALU.mult), reads=[PT[b]], writes=[tat])
                    A(ENG_X1, lambda E, ta=ta, hsl=hsl, xt=xt: E.tensor_tensor(out=xt[:, hsl], in0=xt[:, hsl], in1=ta[:], op=ALU.add), reads=[tat, xtt], writes=[xtt], cost=(1.3 if ENG_X1 == "pool" else 0.6))
                A("sp", lambda E, t=t, xt=xt: E.dma_start(out=x1_s[t * 128:(t + 1) * 128, :], in_=xt[:]), reads=[xtt], writes=[("x1_s", t)], dma=True, cost=1.5)
                norm_tile(xt[:], xtt, 1, cv, xn2[p], "xn2_%d" % p, (ss, rs, xs), p, banks=(nb("p"), nb("p")), all_act=True)
                A("sp", lambda E, p=p, t=t: E.dma_start(out=xn2T_s[:, :, t * 128:(t + 1) * 128], in_=xn2[p][:]), reads=[("xn2_%d" % p, k_) for k_ in range(8)], writes=[("xn2T_s", t)], dma=True)
        S.barrier()


def _ffn(nc, S, A, T, PS, PT, psm, L):
    g_ = L
    xn2T_s, x1_s, y, msk, normf = g_["xn2T_s"], g_["x1_s"], g_["y"], g_["msk"], g_["normf"]
    w_up_v, w_gate_v, w_down_v = g_["w_up_v"], g_["w_gate_v"], g_["w_down_v"]
    identb, vecT, cwT, cwA, epsb = g_["identb"], g_["vecT"], g_["cwT"], g_["cwA"], g_["epsb"]
    PA, PB = 66, 2
    WA, WB = 1024 + 2 * PA, 512 + 2 * PB
    bank = [0]

    def nb():
        b = bank[0] % 7
        bank[0] += 1
        return b

    with ExitStack() as e3:
        hT = T("hT", [128, NFC, NTOK], BF16, e3)
        with ExitStack() as e3a:
            xn = T("xn", [128, 8, NTOK], BF16, e3a)
            wu = [T("wu%d" % i, [128, 8, 256], BF16, e3a) for i in range(3)]
            wg = [T("wg%d" % i, [128, 8, 256], BF16, e3a) for i in range(3)]
            hA = [T("hA%d" % i, [128, 3, WA], BF16, e3a) for i in range(NHB)]
            hB = [T("hB%d" % i, [128, 3, WB], BF16, e3a) for i in range(NHB)]
            mkf = T("mkf", [128, NTOK], F32, e3a)
            mk = T("mk", [128, 2, NTOK], BF16, e3a)
            dg = [T("dg%d" % i, [128, 6, 128], BF16, e3a) for i in range(NHB)]
            accp = [T("accp%d" % i, [128, 2, 512], F32, e3a) for i in range(NHB)]
            sl = [T("sl%d" % i, [128, 512], F32, e3a) for i in range(3)]
            gsb = [T("gsb%d" % i, [128, 512], F32, e3a) for i in range(3)]
            print("ffn-up sbuf remaining", nc.sbuf_bytes_remaining)

            for q in range(3):
                A("sp", lambda E, q=q: E.dma_start(out=xn[:, :, q * 512:(q + 1) * 512], in_=xn2T_s[:, :, q * 512:(q + 1) * 512]),
                  reads=[("xn2T_s", t) for t in range(4 * q, 4 * q + 4)], writes=[("xn", q)], dma=True)
            for i in range(2):
                A("sp", lambda E, i=i: E.dma_start(out=mkf[:], in_=msk[i].partition_broadcast(128)), writes=["mkf"], dma=True)
                A("dve", lambda E, i=i: E.tensor_copy(out=mk[:, i, :], in_=mkf[:]), reads=["mkf"], writes=["mk"])
            for i in range(NHB):
                A("pool", lambda E, i=i: E.memset(hA[i][:], 0.0), writes=[("hA%d" % i, v, gi) for v in range(3) for gi in range(2)])
                A("pool", lambda E, i=i: E.memset(hB[i][:], 0.0), writes=[("hB%d" % i, v, 2) for v in range(3)])
            tgroups = [(0, 512), (512, 512), (1024, 512)]
            sc = 0
            tpc = [0]
            for c in range(NFC):
                pb = c % NHB
                if c % 2 == 0:
                    i = (c // 2) % 3
                    ncol = min(256, DFF - c * 128)
                    A("pool", lambda E, i=i, c=c, ncol=ncol: E.dma_start(out=wu[i][:, :, 0:ncol], in_=w_up_v[:, :, c * 128:c * 128 + ncol]), writes=["wu%d" % i], dma=True, cost=3.0)
                    A("pool", lambda E, i=i, c=c, ncol=ncol: E.dma_start(out=wg[i][:, :, 0:ncol], in_=w_gate_v[:, :, c * 128:c * 128 + ncol]), writes=["wg%d" % i], dma=True, cost=3.0)
                i = (c // 2) % 3
                j = c % 2
                wut, wgt = "wu%d" % i, "wg%d" % i
                for dw in range(6):
                    tp = 3 + dw if dw < 3 else dw - 3
                    col = cwA[:, tp * NFC + c:tp * NFC + c + 1]
                    if dw % 2 == 0:
                        A("dve", lambda E, pb=pb, dw=dw, col=col: E.tensor_scalar(out=dg[pb][:, dw, :], in0=identb[:], scalar1=col, scalar2=None, op0=ALU.mult),
                          writes=[("dg%d" % pb, dw)], cost=0.25)
                    else:
                        A("act", lambda E, pb=pb, dw=dw, col=col: E.activation(out=dg[pb][:, dw, :], in_=identb[:], func=AF.Identity, scale=col),
                          writes=[("dg%d" % pb, dw)], cost=0.3)
                for gi, (t0, n) in enumerate(tgroups):
                    b = nb()
                    for k in range(8):
                        A("pe", lambda E, b=b, k=k, t0=t0, i=i, j=j: E.matmul(PS[b][:], lhsT=wu[i][:, k, j * 128:(j + 1) * 128], rhs=xn[:, k, t0:t0 + 512], start=(k == 0), stop=(k == 7)),
                          reads=[wut, ("xn", gi)], writes=[PT[b]], cost=mmc(512))
                    if gi < 2:
                        buf, bname, o0 = hA[pb], "hA%d" % pb, PA + t0
                    else:
                        buf, bname, o0 = hB[pb], "hB%d" % pb, PB
                    A("act", lambda E, b=b, buf=buf, o0=o0: E.copy(out=buf[:, 1, o0:o0 + 512], in_=PS[b][:]), reads=[PT[b]], writes=[(bname, 1, gi)], cost=0.65)
                    A("dve", lambda E, b=b, buf=buf, o0=o0, t0=t0: E.tensor_tensor(out=buf[:, 0, o0:o0 + 512], in0=PS[b][:], in1=mk[:, 0, t0:t0 + 512], op=ALU.mult), reads=[PT[b], "mk"], writes=[(bname, 0, gi)], cost=0.65)
                    A("dve", lambda E, b=b, buf=buf, o0=o0, t0=t0: E.tensor_tensor(out=buf[:, 2, o0:o0 + 512], in0=PS[b][:], in1=mk[:, 1, t0:t0 + 512], op=ALU.mult), reads=[PT[b], "mk"], writes=[(bname, 2, gi)], cost=0.65)
                for gi, (t0, n) in enumerate(tgroups):
                    bc = nb()
                    if gi < 2:
                        buf, bname, o0 = hA[pb], "hA%d" % pb, PA + t0
                        gis = (0, 1)
                    else:
                        buf, bname, o0 = hB[pb], "hB%d" % pb, PB
                        gis = (2,)
                    petaps = [(0, -1, 0), (0, 0, 1), (0, 1, 2)] + ([(-1, -1, 3), (-1, 0, 4), (-1, 1, 5)] if gi < 2 else [])
                    for n_, (dr, dw, di) in enumerate(petaps):
                        off = o0 + 64 * dr + dw
                        A("pe", lambda E, bc=bc, buf=buf, dw=dw, off=off, pb=pb, n_=n_, di=di, nt=len(petaps): E.matmul(PS[bc][:], lhsT=dg[pb][:, di, :], rhs=buf[:, dw + 1, off:off + 512],
                                                                                                              start=(n_ == 0), stop=(n_ == nt - 1)),
                          reads=[("dg%d" % pb, di)] + [(bname, dw + 1, g2) for g2 in gis], writes=[PT[bc]], cost=mmc(512))
                    bg = nb()
                    for k in range(8):
                        A("pe", lambda E, bg=bg, k=k, t0=t0, i=i, j=j: E.matmul(PS[bg][:], lhsT=wg[i][:, k, j * 128:(j + 1) * 128], rhs=xn[:, k, t0:t0 + 512], start=(k == 0), stop=(k == 7)),
                          reads=[wgt, ("xn", gi)], writes=[PT[bg]], cost=mmc(512))
                    q_ = sc % 3
                    sc += 1
                    s_, stok = sl[q_], "sl%d" % q_
                    gs_, gtok = gsb[q_], "gsb%d" % q_
                    A("act", lambda E, bg=bg, gs_=gs_: E.copy(out=gs_[:], in_=PS[bg][:]), reads=[PT[bg]], writes=[gtok], cost=0.65)
                    if gi < 2:
                        ac = accp[pb][:, gi, :]
                        atok = ("accp%d" % pb, gi)
                        A("act", lambda E, bc=bc, ac=ac: E.copy(out=ac, in_=PS[bc][:]), reads=[PT[bc]], writes=[atok], cost=0.65)
                        for (dr, dw) in [(1, -1), (1, 0), (1, 1)]:
                            tp = (dr + 1) * 3 + (dw + 1)
                            off = o0 + 64 * dr + dw
                            col = cwA[:, tp * NFC + c:tp * NFC + c + 1]
                            src = buf[:, dw + 1, off:off + 512]
                            rd = [(bname, dw + 1, 0), (bname, dw + 1, 1)]
                            A("dve", lambda E, ac=ac, src=src, col=col: E.scalar_tensor_tensor(out=ac, in0=src, scalar=col, in1=ac, op0=ALU.mult, op1=ALU.add),
                              reads=rd + [atok], writes=[atok], cost=0.65)
                        A("act", lambda E, ac=ac, s_=s_, c=c: E.activation(out=s_[:], in_=ac, func=AF.Silu, bias=vecT[:, V_FCB + c:V_FCB + c + 1]), reads=[atok], writes=[stok], cost=0.65)
                    else:
                        A("act", lambda E, bc=bc, s_=s_, c=c: E.activation(out=s_[:], in_=PS[bc][:], func=AF.Silu, bias=vecT[:, V_FCB + c:V_FCB + c + 1]), reads=[PT[bc]], writes=[stok], cost=0.65)
                    A("dve", lambda E, gs_=gs_, s_=s_, c=c, t0=t0: E.tensor_tensor(out=hT[:, c, t0:t0 + 512], in0=gs_[:], in1=s_[:], op=ALU.mult), reads=[gtok, stok], writes=[("hT", c, gi)], cost=0.65)
            S.barrier()
        with ExitStack() as e3b:
            wdr = T("wdr", [128, NFC, 1024], BF16, e3b)
            yb = T("yb", [128, 4, 1024], F32, e3b)
            x1t = [T("x1r%d" % i, [128, 1024], F32, e3b) for i in range(3)]
            ga2 = T("ga2", [128, 2, 1024], F32, e3b)
            nfb = T("nfb", [128, 1024], F32, e3b)
            ss = T("ss3", [128, 4], F32, e3b)
            rs = T("rs3", [128, 4], F32, e3b)
            print("ffn-down sbuf remaining", nc.sbuf_bytes_remaining)
            A("sp", lambda E: E.dma_start(out=nfb[:], in_=normf.partition_broadcast(128)), writes=["nfb"], dma=True)
            A("sp", lambda E: E.dma_start(out=ga2[:], in_=g_["ga2_s"]), writes=["ga2"], dma=True)
            for cb in range(NFC // 2):
                A("pool", lambda E, cb=cb: E.dma_start(out=wdr[:, cb * 2:cb * 2 + 2, :], in_=w_down_v[:, cb * 2:cb * 2 + 2, :]), writes=[("wdr", cb)], dma=True, cost=3.0)

            def acc_of(bi):
                if bi < 7:
                    return PS[bi][:], PT[bi]
                return psm[:], "psm"

            xc = 0
            for ps_ in range(3):
                if ps_ == 0:
                    order = [(c, ti, hf) for c in range(NFC) for ti in range(4) for hf in range(2)]
                else:
                    order = [(c, ti, hf) for ti in range(4) for hf in range(2) for c in range(NFC)]
                for (c, ti, hf) in order:
                    t = ps_ * 4 + ti
                    o, otok = acc_of(ti * 2 + hf)
                    A("pe", lambda E, o=o, c=c, t=t, hf=hf: E.matmul(o, lhsT=hT[:, c, t * 128:(t + 1) * 128], rhs=wdr[:, c, hf * 512:(hf + 1) * 512],
                                                                   start=(c == 0), stop=(c == NFC - 1)), reads=[("hT", c, t // 4), ("wdr", c // 2)], writes=[otok])
                for ti in range(4):
                    t = ps_ * 4 + ti
                    p = xc % 3
                    xc += 1
                    cv = 0 if t < 8 else 1
                    A("sp", lambda E, p=p, t=t: E.dma_start(out=x1t[p][:], in_=x1_s[t * 128:(t + 1) * 128, :]), writes=["x1r%d" % p], dma=True, cost=1.5)
                    for hf in range(2):
                        o, otok = acc_of(ti * 2 + hf)
                        hsl = slice(hf * 512, (hf + 1) * 512)
                        A("dve", lambda E, o=o, ti=ti, hsl=hsl, cv=cv: E.tensor_tensor(out=yb[:, ti, hsl], in0=o, in1=ga2[:, cv, hsl], op=ALU.mult), reads=[otok, "ga2"], writes=[("yb", ti, hf)])
                        A("pool", lambda E, ti=ti, p=p, hsl=hsl: E.tensor_tensor(out=x1t[p][:, hsl], in0=x1t[p][:, hsl], in1=yb[:, ti, hsl], op=ALU.add), reads=[("yb", ti, hf), "x1r%d" % p], writes=["x1r%d" % p])
                    A("act", lambda E, p=p, ti=ti: E.activation(out=yb[:, ti, :], in_=x1t[p][:], func=AF.Square, accum_out=ss[:, p:p + 1]), reads=["x1r%d" % p, ("yb", ti, 0), ("yb", ti, 1)],
                      writes=[("yb", ti, 0), ("yb", ti, 1), "ss3_%d" % p], cost=1.1)
                    A("act", lambda E, p=p: E.activation(out=rs[:, p:p + 1], in_=ss[:, p:p + 1], func=AF.Ln, scale=1.0 / 1024, bias=epsb[:, 0:1]), reads=["ss3_%d" % p, "epsb0"], writes=["rs3_%d" % p], cost=0.25)
                    A("act", lambda E, p=p: E.activation(out=rs[:, p:p + 1], in_=rs[:, p:p + 1], func=AF.Exp, scale=-0.5), reads=["rs3_%d" % p], writes=["rs3_%d" % p], cost=0.25)
                    A("dve", lambda E, p=p: E.scalar_tensor_tensor(out=x1t[p][:], in0=x1t[p][:], scalar=rs[:, p:p + 1], in1=nfb[:], op0=ALU.mult, op1=ALU.mult),
                      reads=["x1r%d" % p, "rs3_%d" % p, "nfb"], writes=["x1r%d" % p], cost=1.2)
                    A("sp", lambda E, p=p, t=t: E.dma_start(out=y[t * 128:(t + 1) * 128, :], in_=x1t[p][:]), reads=["x1r%d" % p], dma=True, cost=1.5)


_CACHE = {}


def _layout(inputs):
    f = lambda a: np.ascontiguousarray(np.asarray(a, dtype=np.float32))
    xp, xs_, c, cctx = f(inputs["x_prompt"]), f(inputs["x_sample"]), f(inputs["c"]), f(inputs["c_ctx"])
    sf, sb = f(inputs["state_gla_fwd"]), f(inputs["state_gla_bwd"])
    vecs = np.concatenate([
        f(inputs["b_ada"])[0].reshape(48, 128), f(inputs["norm1_g"])[0].reshape(8, 128), f(inputs["norm2_g"])[0].reshape(8, 128),
        f(inputs["conv_mix_w"])[0].reshape(24, 128), f(inputs["ffn_conv_b"])[0].reshape(22, 128), f(inputs["gla_norm_g"])[0].reshape(2, 128)], axis=0)
    cwr = f(inputs["ffn_conv_w"])[0].reshape(9 * 22, 128)
    wgk = np.stack([np.concatenate([f(inputs["w_gk_f"])[0], f(inputs["b_gk_f"])], axis=0),
                    np.concatenate([f(inputs["w_gk_b"])[0], f(inputs["b_gk_b"])], axis=0)], axis=1)
    shared = dict(vecs=f(vecs), cwr=cwr, w_ada=f(inputs["w_ada"])[0], b_ada=f(inputs["b_ada"])[0], w_in=f(inputs["w_in"])[0], wgk=f(wgk),
                  w_out=f(inputs["w_out"])[0], w_up=f(inputs["ffn_w_up"])[0], w_gate=f(inputs["ffn_w_gate"])[0], w_down=f(inputs["ffn_w_down"])[0],
                  normf=f(inputs["normf_g"]))
    maps, plan = [], []
    tok = np.arange(NTOK)
    for core in range(8):
        if core < 4:
            pids = [2 * core, 2 * core + 1]
            x = np.concatenate([xs_[core]] + [xp[i] for i in pids], axis=0)
            cv2 = np.stack([c[core], cctx])
            s0f = sf[core, 0].transpose(1, 0, 2).reshape(128, 1024)
            s0b = sb[core, 0].transpose(1, 0, 2).reshape(128, 1024)
            link = 1.0
            w_of = np.where(tok < 1024, tok % 64, tok % 256)
            wmax = np.where(tok < 1024, 63, 255)
            flag = np.ones(9, np.float32)
            plan.append(("s", core, [(4, pids[0]), (5, pids[1])]))
        else:
            pids = [8 + 6 * (core - 4) + i for i in range(6)]
            x = np.concatenate([xp[i] for i in pids], axis=0)
            cv2 = np.stack([cctx, cctx])
            s0f = np.zeros((128, 1024), np.float32)
            s0b = np.zeros((128, 1024), np.float32)
            link = 0.0
            w_of = tok % 256
            wmax = np.full(NTOK, 255)
            flag = np.array([0, 0, 0, 1, 1, 1, 0, 0, 0], np.float32)
            plan.append(("p", core, [(i, pids[i]) for i in range(6)]))
        lnk = np.tile(np.array([[link, -(1.0 - link), -1.0, 1.0]], np.float32), (128, 1))
        msk = np.stack([(w_of != wmax), (w_of != 0)]).astype(np.float32)
        m = dict(xin=f(x), cv2=f(cv2), s0f=f(s0f), s0b=f(s0b), lnk=lnk, msk=msk, flagA=np.tile(flag[None, :], (128, 1)))
        m.update(shared)
        maps.append(m)
    return maps, plan


def kernel(**inputs):
    if "nc" not in _CACHE:
        _CACHE["nc"] = build_program()[0]
    nc = _CACHE["nc"]
    maps, plan = _layout(inputs)
    res = run_bass_kernel_spmd(nc, maps, core_ids=list(range(8)))
    y_prompt = np.zeros((32, 256, 1024), np.float32)
    y_sample = np.zeros((4, 1024, 1024), np.float32)
    nsf = np.zeros((32, 1, 4, 128, 256), np.float32)
    nsb = np.zeros((32, 1, 4, 128, 256), np.float32)
    for (kind, core, slots) in plan:
        r = res.results[core]
        yy = np.asarray(r["y"])
        if kind == "s":
            y_sample[core] = yy[:1024]
        for (slot, pid) in slots:
            y_prompt[pid] = yy[slot * 256:(slot + 1) * 256]
            nsf[pid, 0] = np.asarray(r["sfo"])[slot].reshape(128, 4, 256).transpose(1, 0, 2)
            nsb[pid, 0] = np.asarray(r["sbo"])[slot].reshape(128, 4, 256).transpose(1, 0, 2)
    return (y_prompt, y_sample, nsf, nsb)
```
